# Optimizing a Trainium2 kernel written in Bass

```python
import jax, jax.numpy as jnp
from jax import lax
import numpy as np

D_MODEL = 2048
BATCH = 1
SEQ = 16384
DEPTH = 1
DEC_BATCH = 8
DEC_SEQ = 32
PAST_LEN = 1024

CHUNK = 64
N_ATT_HEADS = 8
HEAD_DIM = 128
ATT_WIDTH = N_ATT_HEADS * HEAD_DIM
N_CONV_GROUPS = 8
CONV_GROUP = 128
CONV_WIDTH = N_CONV_GROUPS * CONV_GROUP
MIX_WIDTH = ATT_WIDTH + CONV_WIDTH
IN_WIDTH = 3 * ATT_WIDTH + 3 * CONV_WIDTH
CONV_K = 3
D_FF = 5632
Q_BLOCK = 128
EPS = 1e-6
SB_SCALE = HEAD_DIM ** -0.5

kernel_name = "hymba_stickbreak_shortconv_convffn_step"


def _rmsnorm(x, g):
    xf = x.astype(jnp.float32)
    y = xf * lax.rsqrt(jnp.mean(xf * xf, axis=-1, keepdims=True) + EPS)
    return (y * g.astype(jnp.float32)).astype(x.dtype)


def _causal_dwconv(u, buf, w):
    T = u.shape[1]
    full = jnp.concatenate([buf.astype(u.dtype), u], axis=1)
    y = full[:, 0:T] * w[0]
    for i in range(1, CONV_K):
        y = y + full[:, i:i + T] * w[i]
    return y, full[:, -(CONV_K - 1):]


def _sb_block(q_blk, k, v, q_pos, k_pos):
    z = jnp.einsum('bqhd,bkhd->bhqk', q_blk, k).astype(jnp.float32) * SB_SCALE
    causal = k_pos[None, :] < q_pos[:, None]
    log_1mb = jnp.where(causal, jax.nn.log_sigmoid(-z), 0.0)
    later = lax.cumsum(log_1mb, axis=3, reverse=True) - log_1mb
    w = jnp.where(causal, jnp.exp(jax.nn.log_sigmoid(z) + later), 0.0)
    return jnp.einsum('bhqk,bkhd->bqhd', w.astype(v.dtype), v)


def _stick_breaking(q, k_all, v_all, past):
    B, T, H, d = q.shape
    k_pos = jnp.arange(k_all.shape[1], dtype=jnp.int32)
    q_pos = past + jnp.arange(T, dtype=jnp.int32)
    if T > Q_BLOCK and T % Q_BLOCK == 0:
        n_blk = T // Q_BLOCK
        qb = q.reshape(B, n_blk, Q_BLOCK, H, d).transpose(1, 0, 2, 3, 4)
        pb = q_pos.reshape(n_blk, Q_BLOCK)
        out = lax.map(lambda a: _sb_block(a[0], k_all, v_all, a[1], k_pos), (qb, pb))
        return out.transpose(1, 0, 2, 3, 4).reshape(B, T, H, d)
    return _sb_block(q, k_all, v_all, q_pos, k_pos)


def _layer(x, cache_k, cache_v, conv_buf, ffn_buf, g_mix, w_in, w_conv, g_att_out,
           g_conv_out, w_o, g_ffn, w_gate_up, w_ffn_conv, w_down):
    B, T, _ = x.shape
    past = cache_k.shape[1]
    h = _rmsnorm(x, g_mix)
    proj = h @ w_in
    splits = [ATT_WIDTH, 2 * ATT_WIDTH, 3 * ATT_WIDTH,
              3 * ATT_WIDTH + CONV_WIDTH, 3 * ATT_WIDTH + 2 * CONV_WIDTH]
    q, k, v, b_gate, c_gate, u = jnp.split(proj, splits, axis=-1)
    q = q.reshape(B, T, N_ATT_HEADS, HEAD_DIM)
    k = k.reshape(B, T, N_ATT_HEADS, HEAD_DIM)
    v = v.reshape(B, T, N_ATT_HEADS, HEAD_DIM)
    k_all = jnp.concatenate([cache_k.astype(k.dtype), k], axis=1)
    v_all = jnp.concatenate([cache_v.astype(v.dtype), v], axis=1)
    att = _stick_breaking(q, k_all, v_all, past)
    att = _rmsnorm(att, g_att_out).reshape(B, T, ATT_WIDTH)
    yc, new_conv = _causal_dwconv(c_gate * u, conv_buf, w_conv)
    yc = (b_gate * yc).reshape(B, T, N_CONV_GROUPS, CONV_GROUP)
    yc = _rmsnorm(yc, g_conv_out).reshape(B, T, CONV_WIDTH)
    x = x + jnp.concatenate([att, yc], axis=-1) @ w_o
    h2 = _rmsnorm(x, g_ffn)
    gate, up = jnp.split(h2 @ w_gate_up, [D_FF], axis=-1)
    gate_c, new_ffn = _causal_dwconv(gate, ffn_buf, w_ffn_conv)
    x = x + (jax.nn.silu(gate_c) * up) @ w_down
    return x, k, v, new_conv, new_ffn


def _trunk(x, cache_k, cache_v, state_conv, state_ffn, g_mix, w_in, w_conv, g_att_out,
           g_conv_out, w_o, g_ffn, w_gate_up, w_ffn_conv, w_down, g_final):
    ks, vs, cs, fs = [], [], [], []
    for l in range(DEPTH):
        x, k, v, c, f = _layer(x, cache_k[l], cache_v[l], state_conv[l], state_ffn[l],
                               g_mix[l], w_in[l], w_conv[l], g_att_out[l], g_conv_out[l],
                               w_o[l], g_ffn[l], w_gate_up[l], w_ffn_conv[l], w_down[l])
        ks.append(k); vs.append(v); cs.append(c); fs.append(f)
    y = _rmsnorm(x, g_final)
    return y, jnp.stack(ks), jnp.stack(vs), jnp.stack(cs), jnp.stack(fs)


def setup_inputs(seed: int = 0) -> dict:
    key = jax.random.key(seed)
    ks = jax.random.split(key, 20)
    f32 = jnp.float32
    nrm = lambda k, s, sc: jax.random.normal(k, s, f32) * sc
    return {
        'x_prompt': nrm(ks[0], (BATCH, SEQ, D_MODEL), 1.0),
        'x_sample': nrm(ks[1], (DEC_BATCH, DEC_SEQ, D_MODEL), 1.0),
        'cache_k': nrm(ks[2], (DEPTH, DEC_BATCH, PAST_LEN, N_ATT_HEADS, HEAD_DIM), 1.0),
        'cache_v': nrm(ks[3], (DEPTH, DEC_BATCH, PAST_LEN, N_ATT_HEADS, HEAD_DIM), 1.0),
        'state_conv': nrm(ks[4], (DEPTH, DEC_BATCH, CONV_K - 1, CONV_WIDTH), 1.0),
        'state_ffn_conv': nrm(ks[5], (DEPTH, DEC_BATCH, CONV_K - 1, D_FF), 1.0),
        'g_mix': 1.0 + nrm(ks[6], (DEPTH, D_MODEL), 0.02),
        'w_in': nrm(ks[7], (DEPTH, D_MODEL, IN_WIDTH), D_MODEL ** -0.5),
        'w_conv': nrm(ks[8], (DEPTH, CONV_K, CONV_WIDTH), CONV_K ** -0.5),
        'g_att_out': 1.0 + nrm(ks[9], (DEPTH, N_ATT_HEADS, HEAD_DIM), 0.02),
        'g_conv_out': 1.0 + nrm(ks[10], (DEPTH, N_CONV_GROUPS, CONV_GROUP), 0.02),
        'w_o': nrm(ks[11], (DEPTH, MIX_WIDTH, D_MODEL), MIX_WIDTH ** -0.5),
        'g_ffn': 1.0 + nrm(ks[12], (DEPTH, D_MODEL), 0.02),
        'w_gate_up': nrm(ks[13], (DEPTH, D_MODEL, 2 * D_FF), D_MODEL ** -0.5),
        'w_ffn_conv': nrm(ks[14], (DEPTH, CONV_K, D_FF), CONV_K ** -0.5),
        'w_down': nrm(ks[15], (DEPTH, D_FF, D_MODEL), D_FF ** -0.5),
        'g_final': 1.0 + nrm(ks[16], (D_MODEL,), 0.02),
    }


def reference(x_prompt, x_sample, cache_k, cache_v, state_conv, state_ffn_conv, g_mix, w_in,
              w_conv, g_att_out, g_conv_out, w_o, g_ffn, w_gate_up, w_ffn_conv, w_down, g_final):
    assert x_sample.shape[1] <= CHUNK
    B, dt = x_prompt.shape[0], x_prompt.dtype
    empty_k = jnp.zeros((DEPTH, B, 0, N_ATT_HEADS, HEAD_DIM), dt)
    zero_conv = jnp.zeros((DEPTH, B, CONV_K - 1, CONV_WIDTH), dt)
    zero_ffn = jnp.zeros((DEPTH, B, CONV_K - 1, D_FF), dt)
    y_prompt, k_p, v_p, c_p, f_p = _trunk(
        x_prompt, empty_k, empty_k, zero_conv, zero_ffn, g_mix, w_in, w_conv, g_att_out,
        g_conv_out, w_o, g_ffn, w_gate_up, w_ffn_conv, w_down, g_final)
    y_sample, k_s, v_s, c_s, f_s = _trunk(
        x_sample, cache_k, cache_v, state_conv, state_ffn_conv, g_mix, w_in, w_conv, g_att_out,
        g_conv_out, w_o, g_ffn, w_gate_up, w_ffn_conv, w_down, g_final)
    return (y_prompt, y_sample, k_p, v_p, c_p, f_p, k_s, v_s, c_s, f_s)
```

```python
import numpy as np
from contextlib import ExitStack
import concourse.bass as bass
import concourse.mybir as mybir
from concourse.bass_utils import run_bass_kernel_spmd

F32 = mybir.dt.float32
BF16 = mybir.dt.bfloat16
AF = mybir.ActivationFunctionType
ALU = mybir.AluOpType

D = 2048
KC = 16
H = 8
HD = 128
DFF = 5632
NF = 44
EPS = 1e-6
SB_SCALE = HD ** -0.5
NCORE = 8
PAST = 1024
DEC = 32
HALO = 4
SAME_ENGINE_SYNC = True


class DSem:
    def __init__(self, sem):
        self.sem = sem
        self.count = 0


class Op:
    __slots__ = ("eng", "fn", "deps", "signals", "value", "sem", "is_dma")


class Buf:
    __slots__ = ("w", "r", "name", "psum")

    def __init__(self, name="", psum=False):
        self.w = []
        self.r = {}
        self.name = name
        self.psum = psum


class Prog:
    ENGS = ("pe", "act", "dve", "pool", "sp")

    def __init__(self, nc, stack):
        self.nc = nc
        self.stack = stack
        self.ops = {e: [] for e in self.ENGS}
        self.esem = {e: DSem(stack.enter_context(nc.semaphore("sem_" + e))) for e in ("pe", "act", "dve", "pool")}
        self.bar_deps = []
        self.bar_seen = set(self.ENGS)
        self.last = {}
        self.dma_open = []
        self.dsems = {}

    def dsem(self, name):
        if name not in self.dsems:
            self.dsems[name] = DSem(self.stack.enter_context(self.nc.semaphore("d_" + name)))
        return self.dsems[name]

    def op(self, eng, fn, reads=(), writes=(), dsem=None):
        o = Op()
        o.eng = eng
        o.fn = fn
        o.is_dma = dsem is not None
        o.signals = False
        o.value = 0
        o.sem = None
        deps = set()
        for b in reads:
            deps.update(b.w)
            if b.psum:
                for en, lst in b.r.items():
                    if en != eng:
                        deps.update(lst)
        for b in writes:
            deps.update(b.w)
            for lst in b.r.values():
                deps.update(lst)
        if eng not in self.bar_seen:
            deps.update(self.bar_deps)
            self.bar_seen.add(eng)
        o.deps = deps
        for b in reads:
            if o.is_dma:
                b.r.setdefault("dma", []).append(o)
            else:
                b.r[eng] = [o]
        for b in writes:
            b.w = [o]
            b.r = {}
        if dsem is not None:
            dsem.count += 16
            o.sem = dsem
            o.value = dsem.count
            o.signals = True
            self.dma_open.append(o)
        else:
            self.last[eng] = o
        self.ops[eng].append(o)
        return o

    def barrier(self):
        self.bar_deps = list(self.last.values()) + list(self.dma_open)
        self.dma_open = []
        self.bar_seen = set()

    @staticmethod
    def needs_wait(o, d):
        if d.is_dma:
            return True
        if d.eng == o.eng:
            if o.is_dma:
                return True
            if o.eng == "pe":
                return False
            return SAME_ENGINE_SYNC
        return True

    def finalize(self):
        for e in self.ENGS:
            for o in self.ops[e]:
                for d in o.deps:
                    if self.needs_wait(o, d):
                        d.signals = True
        for e in ("pe", "act", "dve", "pool"):
            cnt = 0
            for o in self.ops[e]:
                if o.is_dma:
                    continue
                if o.signals:
                    cnt += 1
                    o.value = cnt
                    o.sem = self.esem[e]

    def emit(self, eng_name, e):
        waited = {}
        for o in self.ops[eng_name]:
            need = {}
            for d in o.deps:
                if not self.needs_wait(o, d):
                    continue
                k = id(d.sem)
                if need.get(k, (None, 0))[1] < d.value:
                    need[k] = (d.sem, d.value)
            for k, (ds, val) in need.items():
                if waited.get(k, 0) < val:
                    e.wait_ge(ds.sem, val)
                    waited[k] = val
            ins = o.fn(e)
            if o.signals:
                ins.then_inc(o.sem.sem, 16 if o.is_dma else 1)

    def final_waits(self, e):
        seen = {}
        for en in self.ENGS:
            for o in self.ops[en]:
                if o.is_dma:
                    k = id(o.sem)
                    if seen.get(k, (None, 0))[1] < o.value:
                        seen[k] = (o.sem, o.value)
        for k, (ds, val) in seen.items():
            e.wait_ge(ds.sem, val)


class Arena:
    def __init__(self, nc, nbytes):
        self.h = nc.alloc_sbuf_tensor("arena", [128, nbytes // 4], F32)
        self.ap = self.h.ap()
        self.off = 0
        self.cap = nbytes
        self.limit = nbytes

    def alloc(self, free_shape, dtype, top=False):
        esz = 4 if dtype == F32 else 2
        n = int(np.prod(free_shape))
        nb = (n * esz + 31) // 32 * 32
        if top:
            off = self.cap - nb
        else:
            assert self.off + nb <= self.limit, "SBUF arena overflow: need %d have %d" % (self.off + nb, self.limit)
            off = self.off
            self.off += nb
        a = self.ap[:, off // 4:(off + nb) // 4]
        if dtype != F32:
            a = a.bitcast(dtype)
        a = a[:, 0:n]
        if len(free_shape) == 2:
            a = a.rearrange("p (a b) -> p a b", a=free_shape[0])
        elif len(free_shape) == 3:
            a = a.rearrange("p (a b c) -> p a b c", a=free_shape[0], b=free_shape[1])
        return a

    def mark(self):
        return self.off

    def reset(self, m):
        self.off = m


class T:
    def __init__(self, ap, name="", dsem=None, psum=False):
        self.cg = None
        self.ap = ap
        self.buf = Buf(name, psum)
        self.dsem = dsem


def col_groups(w):
    if w <= 512:
        return [(0, w)]
    return [(0, 512), (512, w)]


def build(cfg):
    NSLOT = cfg["nslot"]
    NT = 8 * NSLOT
    NTOK = NT * 512
    W = 512 + HALO
    WS = DEC + HALO
    nf = cfg.get("nf", NF)
    dff = nf * 128

    nc = bass.Bass("TRN2", target_bir_lowering=False)
    stack = ExitStack()

    def din(name, shape, dt=F32):
        return nc.dram_tensor(name, list(shape), dt, kind="ExternalInput").ap()

    def dout(name, shape, dt=F32):
        return nc.dram_tensor(name, list(shape), dt, kind="ExternalOutput").ap()

    x_seq = din("x_seq", [NTOK, D])
    x_own = din("x_own", [NSLOT, W, D])
    x_smp = din("x_smp", [WS, D])
    cache_k = din("cache_k", [PAST, H, HD])
    cache_v = din("cache_v", [PAST, H, HD])
    st_conv = din("st_conv", [128, 8, 2])
    st_ffn = din("st_ffn", [128, nf, 2])
    w_in = din("w_in", [D, 6 * 1024])
    w_o = din("w_o", [D, D])
    w_gu = din("w_gu", [D, 2 * dff])
    w_dn = din("w_dn", [dff, D])
    c_bf = din("c_bf", [128, 3 * 128 + 5 * W + WS])
    c_f32 = din("c_f32", [128, 128 + 2 * D + KC + 8 + 8 + 24 + 3 * nf])

    y_own = dout("y_own", [NSLOT, 512, D])
    y_smp = dout("y_smp", [DEC, D])
    k_own = dout("k_own", [NSLOT, 512, 1024])
    v_own = dout("v_own", [NSLOT, 512, 1024])
    k_smp = dout("k_smp", [DEC, 1024])
    v_smp = dout("v_smp", [DEC, 1024])
    conv_last = dout("conv_last", [NSLOT + 1, 128, 16])
    ffn_last = dout("ffn_last", [NSLOT + 1, 128, nf * 2])

    DEBUG = cfg.get("debug", False)
    if DEBUG:
        dbg_mix = dout("dbg_mix", [NSLOT + 1, 128, 16 * W], BF16)
        dbg_xmid = dout("dbg_xmid", [NSLOT + 1, 128, 16 * W], F32)
        dbg_h2 = dout("dbg_h2", [NSLOT + 1, 128, 16 * W], BF16)
        dbg_att = dout("dbg_att", [NSLOT + 1, 128, 8 * W], F32)
        dbg_nrm = dout("dbg_nrm", [NSLOT + 1, 128, 8 * W], F32)
    kT_scr = nc.dram_tensor("kT_scr", [H, 128, NTOK], BF16).ap()
    v_scr = nc.dram_tensor("v_scr", [NTOK, 1024], BF16).ap()

    NTA = 8 + nf
    NTB = 4 * (KC // 4 + nf // 4)
    wsA = nc.dram_tensor("wsA", [NTA, 128, KC * 256], BF16).ap()
    wsB = nc.dram_tensor("wsB", [NTB, 128, 4 * 512], BF16).ap()
    wsA_buf = [Buf("wsA%d" % i) for i in range(NTA)]
    wsB_buf = [Buf("wsB%d" % i) for i in range(NTB)]
    P = Prog(nc, stack)
    AR = Arena(nc, 207 * 1024)
    PSH = nc.alloc_psum_tensor("psum_all", [128, 4096], F32)
    PS = PSH.ap()

    def ps_f32(bank, nb=1):
        return PS[:, bank * 512:(bank + nb) * 512]

    def ps_bf(bank, nb=1):
        return PS[:, bank * 512:(bank + nb) * 512].bitcast(BF16)

    cb = T(AR.alloc([3 * 128 + 5 * W + WS], BF16), "cb", P.dsem("cb"))
    cf = T(AR.alloc([128 + 2 * D + KC + 8 + 8 + 24 + 3 * nf], F32), "cf", P.dsem("cf"))
    ident_bf = cb.ap[:, 0:128]
    ones_bf = cb.ap[:, 128:256]
    negtri = cb.ap[:, 256:384]
    masks = [cb.ap[:, 384 + i * W:384 + (i + 1) * W] for i in range(5)]
    mask_s = cb.ap[:, 384 + 5 * W:384 + 5 * W + WS]
    o = 0
    ident_f = cf.ap[:, o:o + 128]; o += 128
    g_mix_rep = cf.ap[:, o:o + D]; o += D
    g_fin_rep = cf.ap[:, o:o + D]; o += D
    g_ffn_pp = cf.ap[:, o:o + KC]; o += KC
    g_att_pp = cf.ap[:, o:o + 8]; o += 8
    g_conv_pp = cf.ap[:, o:o + 8]; o += 8
    wconv_pp = cf.ap[:, o:o + 24]; o += 24
    wffn_pp = cf.ap[:, o:o + 3 * nf]; o += 3 * nf

    P.op("pool", lambda e: e.dma_start(out=cb.ap, in_=c_bf), writes=[cb.buf], dsem=cb.dsem)
    P.op("sp", lambda e: e.dma_start(out=cf.ap, in_=c_f32), writes=[cf.buf], dsem=cf.dsem)
    CONST = [cb.buf, cf.buf]

    mixT = [None]
    persist_mark = AR.mark()

    def rmsnorm_rows(xs, rows, xn, tmp, g_rep):
        junk, ss, rs = tmp
        P.op("dve", lambda e: e.memset(ss.ap[0:rows, 0:1], 0.0), writes=[ss.buf])
        P.op("act", lambda e: e.activation(out=junk.ap[0:rows, :], in_=xs.ap[0:rows, :], func=AF.Square,
                                           accum_out=ss.ap[0:rows, 0:1]),
             reads=[xs.buf], writes=[junk.buf, ss.buf])
        P.op("act", lambda e: e.activation(out=rs.ap[0:rows, 0:1], in_=ss.ap[0:rows, 0:1], func=AF.Sqrt,
                                           scale=1.0 / D, bias=EPS),
             reads=[ss.buf], writes=[rs.buf])
        P.op("dve", lambda e: e.reciprocal(out=rs.ap[0:rows, 0:1], in_=rs.ap[0:rows, 0:1]),
             reads=[rs.buf], writes=[rs.buf])
        P.op("dve", lambda e: e.scalar_tensor_tensor(out=xn.ap[0:rows, :], in0=xs.ap[0:rows, :],
                                                     scalar=rs.ap[0:rows, 0:1], in1=g_rep[0:rows, :],
                                                     op0=ALU.mult, op1=ALU.mult),
             reads=[xs.buf, rs.buf] + CONST, writes=[xn.buf])

    def mm_group(out_ap, pairs, lo, hi):
        def fn(e):
            ins = None
            n = len(pairs)
            for i, (l, r) in enumerate(pairs):
                ins = e.matmul(out_ap[:, lo:hi], l, r[:, lo:hi], start=(i == 0), stop=(i == n - 1))
            return ins
        return fn

    def phase_A():
        m0 = AR.mark()
        wk = T(AR.alloc([KC, 1024], BF16), "wk", P.dsem("wk"))
        wv = T(AR.alloc([KC, 1024], BF16), "wv", P.dsem("wv"))
        for (wt, c0) in ((wk, 1024), (wv, 2048)):
            wops = []
            for half in range(2):
                src = w_in[:, c0 + half * 512:c0 + (half + 1) * 512].rearrange("(k p) c -> p k c", p=128)
                wops.append(P.op("pool", (lambda wt=wt, half=half, src=src:
                                          lambda e: e.dma_start(out=wt.ap[:, :, half * 512:(half + 1) * 512], in_=src))(),
                                 writes=[], dsem=wt.dsem))
            wt.buf.w = [wops[-1]]
        XS = [T(AR.alloc([D], F32), "xs%d" % i, P.dsem("xs%d" % i)) for i in range(3)]
        XN = [T(AR.alloc([D], BF16), "xn%d" % i) for i in range(2)]
        junk = T(AR.alloc([D], BF16), "junk")
        ss = T(AR.alloc([8], F32), "ss")
        rs = T(AR.alloc([8], F32), "rs")
        HT = [T(AR.alloc([KC, 512], BF16), "hT%d" % i) for i in range(2)]
        HTB = [[Buf("htb") for j in range(4)] for i in range(2)]
        KST = [[T(AR.alloc([512], BF16), "kst", P.dsem("kst%d_%d" % (i, h))) for h in range(H)] for i in range(2)]
        VST = [[T(AR.alloc([1024], BF16), "vst", P.dsem("vst%d_%d" % (i, j))) for j in range(4)] for i in range(2)]
        VSTH = [[[Buf("vsth") for cg in range(2)] for j in range(4)] for i in range(2)]
        TP = [T(ps_bf(2 * i, 2), "tp%d" % i, psum=True) for i in range(2)]
        KP = [T(ps_f32(4 + i), "kp%d" % i, psum=True) for i in range(2)]
        VP = [T(ps_f32(6 + i), "vp%d" % i, psum=True) for i in range(2)]
        kbufs, vbufs = [None] * (NT * H), [None] * (NT * 4)
        ev = [0]

        def prepA(i, j):
            n = i * 4 + j
            xs, xn = XS[n % 3], XN[n % 2]
            P.op("sp", lambda e: e.dma_start(out=xs.ap, in_=x_seq[n * 128:(n + 1) * 128, :]), writes=[xs.buf], dsem=xs.dsem)
            rmsnorm_rows(xs, 128, xn, (junk, ss, rs), g_mix_rep)

        def prepB(i, j):
            n = i * 4 + j
            xn, tp = XN[n % 2], TP[n % 2]
            ht = HT[i % 2]

            def tr(e):
                ins = None
                for kc in range(KC):
                    ins = e.transpose(tp.ap[:, kc * 128:(kc + 1) * 128], xn.ap[:, kc * 128:(kc + 1) * 128], ident_bf)
                return ins
            P.op("pe", tr, reads=[xn.buf] + CONST, writes=[tp.buf])
            P.op("act", lambda e: e.activation(out=ht.ap[:, :, j * 128:(j + 1) * 128],
                                               in_=tp.ap.rearrange("p (k t) -> p k t", k=KC), func=AF.Copy),
                 reads=[tp.buf], writes=[HTB[i % 2][j]])

        def evac(dst_ap, src, wbuf):
            if ev[0] % 2 == 0:
                P.op("dve", lambda e: e.tensor_copy(out=dst_ap, in_=src.ap), reads=[src.buf], writes=[wbuf])
            else:
                P.op("act", lambda e: e.activation(out=dst_ap, in_=src.ap, func=AF.Copy), reads=[src.buf], writes=[wbuf])
            ev[0] += 1

        def kgroup(i, h):
            ht = HT[i % 2]
            kp = KP[h % 2]
            P.op("pe", lambda e: [e.matmul(kp.ap, wk.ap[:, kc, h * 128:(h + 1) * 128], ht.ap[:, kc, :],
                                           start=(kc == 0), stop=(kc == KC - 1)) for kc in range(KC)][-1],
                 reads=[wk.buf] + HTB[i % 2], writes=[kp.buf])
            kst = KST[i % 2][h]
            evac(kst.ap, kp, kst.buf)
            kb = Buf("kscr")
            P.op("pool", lambda e: e.dma_start(out=kT_scr[h, :, i * 512:(i + 1) * 512], in_=kst.ap),
                 reads=[kst.buf], writes=[kb], dsem=kst.dsem)
            kbufs[i * H + h] = kb

        def vgroup(i, j, cg):
            ht = HT[i % 2]
            vp = VP[cg]
            P.op("pe", lambda e: [e.matmul(vp.ap, ht.ap[:, kc, j * 128:(j + 1) * 128], wv.ap[:, kc, cg * 512:(cg + 1) * 512],
                                           start=(kc == 0), stop=(kc == KC - 1)) for kc in range(KC)][-1],
                 reads=[wv.buf, HTB[i % 2][j]], writes=[vp.buf])
            vst = VST[i % 2][j]
            evac(vst.ap[:, cg * 512:(cg + 1) * 512], vp, VSTH[i % 2][j][cg])
            if cg == 1:
                vb = Buf("vscr")
                P.op("pool", lambda e: e.dma_start(out=v_scr[i * 512 + j * 128:i * 512 + (j + 1) * 128, :], in_=vst.ap),
                     reads=VSTH[i % 2][j], writes=[vb], dsem=vst.dsem)
                vbufs[i * 4 + j] = vb

        subs = [(i, j) for i in range(NT) for j in range(4)]
        prepA(0, 0)
        for n in range(4):
            prepB(*subs[n])
            prepA(*subs[n + 1])
        nxt = 4
        for i in range(NT):
            groups = [("k", h) for h in range(H)] + [("v", j, cg) for j in range(4) for cg in range(2)]
            for gi, g in enumerate(groups):
                if g[0] == "k":
                    kgroup(i, g[1])
                else:
                    vgroup(i, g[1], g[2])
                if gi % 4 == 1 and nxt < len(subs):
                    prepB(*subs[nxt])
                    if nxt + 1 < len(subs):
                        prepA(*subs[nxt + 1])
                    nxt += 1
        P.barrier()
        AR.reset(m0)
        return kbufs, vbufs

    def slot(si, is_smp, kbufs, vbufs):
        Wd = WS if is_smp else W
        CG = col_groups(Wd)
        nown = DEC if is_smp else 512
        xsrc = x_smp if is_smp else x_own[si]
        subt = [(r0, min(128, Wd - r0)) for r0 in range(0, Wd, 128)]
        osub = [(r0, min(128, nown - r0)) for r0 in range(0, nown, 128)]
        m_slot = AR.mark()
        mix = T(AR.alloc([16, Wd], BF16), "mixT")
        mixh = [Buf("mixh%d" % i) for i in range(16)]
        qT = T(AR.alloc([H, Wd], BF16), "qT")
        clast = T(AR.alloc([16], F32), "clast")
        flast = T(AR.alloc([nf * 2], F32), "flast")
        stc = T(AR.alloc([8, 2], F32), "stc", P.dsem("stc"))
        stf = T(AR.alloc([nf, 2], F32), "stf", P.dsem("stf"))
        kTs = T(AR.alloc([H, 128], BF16), "kTs")
        vs = T(AR.alloc([H, 128], BF16), "vs")
        m_low = AR.mark()
        xT = T(AR.alloc([KC, Wd], F32), "xT")
        m_x = AR.mark()
        hTo = T(AR.alloc([KC, Wd], BF16), "hTo")
        actT = T(AR.alloc([nf, nown], BF16, top=True), "actT")
        act_bytes = (nf * nown * 2 + 31) // 32 * 32
        if is_smp:
            P.op("sp", lambda e: e.dma_start(out=stc.ap, in_=st_conv), writes=[stc.buf], dsem=stc.dsem)
            P.op("sp", lambda e: e.dma_start(out=stf.ap, in_=st_ffn), writes=[stf.buf], dsem=stf.dsem)
        m_s1 = AR.mark()

        XS = [T(AR.alloc([D], F32), "xso%d" % i, P.dsem("xso%d" % i)) for i in range(2)]
        XN = [T(AR.alloc([D], BF16), "xno%d" % i) for i in range(2)]
        junk = T(AR.alloc([D], BF16), "junk")
        ss = T(AR.alloc([8], F32), "ss")
        rs = T(AR.alloc([8], F32), "rs")
        TPb = [T(ps_bf(2 * i, 2), "tpb%d" % i, psum=True) for i in range(2)]
        TPf = [T(ps_f32(4 + 2 * i, 2), "tpf%d" % i, psum=True) for i in range(2)]
        hsub, xsub = [], []
        for n, (r0, nr) in enumerate(subt):
            xs, xn, tpb = XS[n % 2], XN[n % 2], TPb[n % 2]
            P.op("sp", (lambda xs=xs, r0=r0, nr=nr: lambda e: e.dma_start(out=xs.ap[0:nr, :], in_=xsrc[r0:r0 + nr, :]))(),
                 writes=[xs.buf], dsem=xs.dsem)
            rmsnorm_rows(xs, nr, xn, (junk, ss, rs), g_mix_rep)

            def trb(e, xn=xn, tpb=tpb, nr=nr):
                ins = None
                for kc in range(KC):
                    ins = e.transpose(tpb.ap[:, kc * 128:kc * 128 + nr], xn.ap[0:nr, kc * 128:(kc + 1) * 128],
                                      ident_bf[0:nr, 0:nr])
                return ins
            P.op("pe", trb, reads=[xn.buf] + CONST, writes=[tpb.buf])
            hb = Buf("hsub")
            P.op("act", (lambda tpb=tpb, r0=r0, nr=nr: lambda e: e.activation(
                out=hTo.ap[:, :, r0:r0 + nr], in_=tpb.ap.rearrange("p (k t) -> p k t", k=KC)[:, :, 0:nr], func=AF.Copy))(),
                reads=[tpb.buf], writes=[hb])
            hsub.append(hb)
            for half in range(2):
                tpf = TPf[half]

                def trf(e, xs=xs, tpf=tpf, nr=nr, half=half):
                    ins = None
                    for k8 in range(8):
                        kc = half * 8 + k8
                        ins = e.transpose(tpf.ap[:, k8 * 128:k8 * 128 + nr], xs.ap[0:nr, kc * 128:(kc + 1) * 128],
                                          ident_f[0:nr, 0:nr])
                    return ins
                P.op("pe", trf, reads=[xs.buf] + CONST, writes=[tpf.buf])
                xb = Buf("xsub")
                P.op("dve", (lambda tpf=tpf, r0=r0, nr=nr, half=half: lambda e: e.tensor_copy(
                    out=xT.ap[:, half * 8:(half + 1) * 8, r0:r0 + nr],
                    in_=tpf.ap.rearrange("p (k t) -> p k t", k=8)[:, :, 0:nr]))(),
                    reads=[tpf.buf], writes=[xb])
                xsub.append(xb)
        P.barrier()
        AR.reset(m_s1)

        WR = [T(AR.alloc([KC, 256], BF16), "wr%d" % i, P.dsem("wr%d" % i)) for i in range(6)]
        wr_i = [0]

        def load_w(src_cols):
            t = WR[wr_i[0] % 6]
            wr_i[0] += 1
            P.op("pool", (lambda t=t, src=src_cols: lambda e: e.dma_start(out=t.ap, in_=src.rearrange("(k p) c -> p k c", p=128)))(),
                 writes=[t.buf], dsem=t.dsem)
            return t
        PJ = [T(ps_f32(2 * i, 2), "pj%d" % i, psum=True) for i in range(4)]
        u_sb = T(AR.alloc([Wd], F32), "u_sb")
        cu = T(AR.alloc([Wd], F32), "cu")
        ycv = T(AR.alloc([Wd], F32), "ycv")
        sq = T(AR.alloc([Wd], BF16), "sq")
        rstd = T(AR.alloc([Wd], F32), "rstd")

        def proj(pj, wt, c0):
            for (lo, hi) in CG:
                P.op("pe", mm_group(pj.ap, [(wt.ap[:, kc, c0:c0 + 128], hTo.ap[:, kc, :]) for kc in range(KC)], lo, hi),
                     reads=[wt.buf] + hsub, writes=[pj.buf] if lo == 0 else [])
            pj.buf.w = [P.ops["pe"][-1]]

        def group_norm_to_mix(src, lo_c, gcol, mrow, mbuf):
            nrm = PJ[3]
            P.op("act", lambda e: e.activation(out=sq.ap[:, lo_c:Wd], in_=src.ap[:, lo_c:Wd], func=AF.Square),
                 reads=[src.buf], writes=[sq.buf])
            for (lo, hi) in CG:
                l2 = max(lo, lo_c)
                P.op("pe", (lambda l2=l2, hi=hi: lambda e: e.matmul(nrm.ap[:, l2:hi], ones_bf, sq.ap[:, l2:hi], start=True, stop=True))(),
                     reads=[sq.buf] + CONST, writes=[nrm.buf] if lo == 0 else [])
            nrm.buf.w = [P.ops["pe"][-1]]
            P.op("act", lambda e: e.activation(out=rstd.ap[:, lo_c:Wd], in_=nrm.ap[:, lo_c:Wd], func=AF.Sqrt,
                                               scale=1.0 / 128, bias=EPS), reads=[nrm.buf], writes=[rstd.buf])
            P.op("dve", lambda e: e.reciprocal(out=rstd.ap[:, lo_c:Wd], in_=rstd.ap[:, lo_c:Wd]),
                 reads=[rstd.buf], writes=[rstd.buf])
            P.op("dve", lambda e: e.scalar_tensor_tensor(out=mix.ap[:, mrow, lo_c:Wd], in0=src.ap[:, lo_c:Wd], scalar=gcol,
                                                         in1=rstd.ap[:, lo_c:Wd], op0=ALU.mult, op1=ALU.mult),
                 reads=[src.buf, rstd.buf] + CONST, writes=[mbuf])

        for gp in range(4):
            wC = load_w(w_in[:, 4096 + gp * 256:4096 + (gp + 1) * 256])
            wU = load_w(w_in[:, 5120 + gp * 256:5120 + (gp + 1) * 256])
            wB = load_w(w_in[:, 3072 + gp * 256:3072 + (gp + 1) * 256])
            for gi in range(2):
                g = gp * 2 + gi
                pC, pU, pB = PJ[0], PJ[1], PJ[2]
                proj(pC, wC, gi * 128)
                proj(pU, wU, gi * 128)
                proj(pB, wB, gi * 128)
                P.op("act", lambda e: e.activation(out=u_sb.ap, in_=pU.ap[:, 0:Wd], func=AF.Copy),
                     reads=[pU.buf], writes=[u_sb.buf])
                P.op("dve", lambda e: e.tensor_tensor(out=cu.ap, in0=pC.ap[:, 0:Wd], in1=u_sb.ap, op=ALU.mult),
                     reads=[pC.buf, u_sb.buf], writes=[cu.buf])
                if is_smp:
                    P.op("dve", (lambda g=g: lambda e: e.tensor_copy(out=cu.ap[:, 2:4], in_=stc.ap[:, g, :]))(),
                         reads=[stc.buf, cu.buf], writes=[cu.buf])
                P.op("dve", (lambda g=g: lambda e: e.tensor_copy(out=clast.ap[:, g * 2:g * 2 + 2], in_=cu.ap[:, Wd - 2:Wd]))(),
                     reads=[cu.buf], writes=[clast.buf])
                P.op("dve", (lambda g=g: lambda e: e.tensor_scalar(out=ycv.ap[:, 2:Wd], in0=cu.ap[:, 2:Wd],
                                                                  scalar1=wconv_pp[:, g * 3 + 2:g * 3 + 3], scalar2=None,
                                                                  op0=ALU.mult))(),
                     reads=[cu.buf] + CONST, writes=[ycv.buf])
                for tap in (1, 0):
                    P.op("dve", (lambda g=g, tap=tap: lambda e: e.scalar_tensor_tensor(
                        out=ycv.ap[:, 2:Wd], in0=cu.ap[:, tap:Wd - 2 + tap], scalar=wconv_pp[:, g * 3 + tap:g * 3 + tap + 1],
                        in1=ycv.ap[:, 2:Wd], op0=ALU.mult, op1=ALU.add))(),
                        reads=[cu.buf, ycv.buf] + CONST, writes=[ycv.buf])
                P.op("dve", lambda e: e.tensor_tensor(out=ycv.ap[:, 2:Wd], in0=pB.ap[:, 2:Wd], in1=ycv.ap[:, 2:Wd], op=ALU.mult),
                     reads=[pB.buf, ycv.buf], writes=[ycv.buf])
                group_norm_to_mix(ycv, 2, g_conv_pp[:, g:g + 1], 8 + g, mixh[8 + g])
        P.op("sp", lambda e: e.dma_start(out=conv_last[si], in_=clast.ap), reads=[clast.buf], dsem=P.dsem("clo"))
        for hp in range(4):
            wQ = load_w(w_in[:, hp * 256:(hp + 1) * 256])
            for hi_ in range(2):
                h = hp * 2 + hi_
                pq = PJ[h % 3]
                proj(pq, wQ, hi_ * 128)
                P.op("act", (lambda h=h, pq=pq: lambda e: e.activation(out=qT.ap[:, h, :], in_=pq.ap[:, 0:Wd], func=AF.Copy, scale=SB_SCALE))(),
                     reads=[pq.buf], writes=[qT.buf] if h == 0 else [])
                if h > 0:
                    qT.buf.w.append(P.ops["act"][-1])
        kvst = [T(AR.alloc([256], F32), "kvst%d" % i, P.dsem("kvst%d" % i)) for i in range(2)]
        if is_smp:
            P.op("pool", lambda e: e.memset(kTs.ap, 0.0), writes=[kTs.buf])
            P.op("pool", lambda e: e.memset(vs.ap, 0.0), writes=[vs.buf])
        kvi = 0
        for (which, c0, dst) in (("k", 1024, k_smp if is_smp else k_own[si]), ("v", 2048, v_smp if is_smp else v_own[si])):
            for q4 in range(4):
                wt = load_w(w_in[:, c0 + q4 * 256:c0 + (q4 + 1) * 256])
                for (r0, nr) in osub:
                    pj = PJ[kvi % 3]
                    st = kvst[kvi % 2]
                    kvi += 1
                    P.op("pe", (lambda pj=pj, wt=wt, r0=r0, nr=nr: lambda e: [e.matmul(
                        pj.ap[0:nr, 0:256], hTo.ap[:, kc, HALO + r0:HALO + r0 + nr], wt.ap[:, kc, :],
                        start=(kc == 0), stop=(kc == KC - 1)) for kc in range(KC)][-1])(),
                        reads=[wt.buf] + hsub, writes=[pj.buf])
                    P.op("act", (lambda pj=pj, st=st, nr=nr: lambda e: e.activation(out=st.ap[0:nr, :], in_=pj.ap[0:nr, 0:256], func=AF.Copy))(),
                         reads=[pj.buf], writes=[st.buf])
                    if is_smp and which == "v":
                        P.op("dve", (lambda st=st, q4=q4, nr=nr: lambda e: e.tensor_copy(
                            out=vs.ap[0:nr, 2 * q4:2 * q4 + 2, :], in_=st.ap[0:nr, :].rearrange("p (a b) -> p a b", a=2)))(),
                            reads=[st.buf, vs.buf], writes=[vs.buf])
                    P.op("sp", (lambda st=st, r0=r0, nr=nr, q4=q4, dst=dst: lambda e: e.dma_start(
                        out=dst[r0:r0 + nr, q4 * 256:(q4 + 1) * 256], in_=st.ap[0:nr, :]))(),
                        reads=[st.buf], dsem=st.dsem)
                if is_smp and which == "k":
                    for hi_ in range(2):
                        h = q4 * 2 + hi_
                        pj = PJ[3]
                        P.op("pe", (lambda pj=pj, wt=wt, hi_=hi_: lambda e: [e.matmul(
                            pj.ap[:, 0:DEC], wt.ap[:, kc, hi_ * 128:(hi_ + 1) * 128], hTo.ap[:, kc, HALO:HALO + DEC],
                            start=(kc == 0), stop=(kc == KC - 1)) for kc in range(KC)][-1])(),
                            reads=[wt.buf] + hsub, writes=[pj.buf])
                        P.op("dve", (lambda pj=pj, h=h: lambda e: e.tensor_copy(out=kTs.ap[:, h, 0:DEC], in_=pj.ap[:, 0:DEC]))(),
                             reads=[pj.buf, kTs.buf], writes=[kTs.buf])
        P.barrier()

        m_att = m_x
        AR.reset(m_x)
        E = T(AR.alloc([Wd], F32), "E")
        SP = [T(AR.alloc([Wd], BF16), "sp%d" % i) for i in range(3)]
        TMP = [T(AR.alloc([Wd], F32), "tmp%d" % i) for i in range(2)]
        WW = [T(AR.alloc([Wd], BF16), "ww%d" % i) for i in range(4)]
        RACC = T(AR.alloc([Wd], F32), "racc")
        def ptile(k, name):
            t = T(ps_f32(2 * k, 2), name, psum=True)
            t.cg = CG
            return t
        S = [ptile(i, "S%d" % i) for i in range(2)]
        OB = ptile(2, "OB")
        OUT = ptile(3, "OUT")
        sqa = T(AR.alloc([Wd], BF16), "sqa")
        rstda = T(AR.alloc([Wd], F32), "rstda")
        osb = T(AR.alloc([Wd], F32), "osb")
        if is_smp:
            ck = T(AR.alloc([8, H * 128], BF16), "ck", P.dsem("ck"))
            cv = T(AR.alloc([8, H * 128], BF16), "cv", P.dsem("cv"))
            kTc = T(AR.alloc([H, PAST], BF16), "kTc")
            P.op("pool", lambda e: e.dma_start(out=ck.ap, in_=cache_k.rearrange("(b p) h d -> p b (h d)", p=128)),
                 writes=[ck.buf], dsem=ck.dsem)
            P.op("pool", lambda e: e.dma_start(out=cv.ap, in_=cache_v.rearrange("(b p) h d -> p b (h d)", p=128)),
                 writes=[cv.buf], dsem=cv.dsem)
            tpk = T(ps_bf(0, 1), "tpk", psum=True)
            for h in range(H):
                def trk(e, h=h):
                    ins = None
                    for b in range(8):
                        ins = e.transpose(tpk.ap[:, b * 128:(b + 1) * 128], ck.ap[:, b, h * 128:(h + 1) * 128], ident_bf)
                    return ins
                P.op("pe", trk, reads=[ck.buf] + CONST, writes=[tpk.buf])
                P.op("dve", (lambda h=h: lambda e: e.tensor_copy(out=kTc.ap[:, h, :], in_=tpk.ap))(),
                     reads=[tpk.buf], writes=[kTc.buf] if h == 0 else [])
                if h > 0:
                    kTc.buf.w.append(P.ops["dve"][-1])
            P.barrier()
            KCH = VCH = None
        else:
            KCH = [T(AR.alloc([2048], BF16), "kch%d" % i, P.dsem("kch%d" % i)) for i in range(3)]
            VCH = [T(AR.alloc([16, 128], BF16), "vch%d" % i, P.dsem("vch%d" % i)) for i in range(3)]
        chunk_i = [0]
        G = []
        loads = []
        first_of_chunk = {}
        for h in range(H):
            blocks = []
            if is_smp:
                blocks.append((kTs.ap[:, h, :], vs.ap[:, h, :], mask_s, [kTs.buf, vs.buf]))
                for b_ in range(7, -1, -1):
                    blocks.append((kTc.ap[:, h, b_ * 128:(b_ + 1) * 128], cv.ap[:, b_, h * 128:(h + 1) * 128], None,
                                   [kTc.buf, cv.buf]))
            else:
                NK = 4096 * (si + 1)
                q0 = NK - 512
                for ch in range(NK // 2048 - 1, -1, -1):
                    kt = KCH[chunk_i[0] % 3]
                    vt = VCH[chunk_i[0] % 3]
                    chunk_i[0] += 1
                    tiles = range(ch * 4, ch * 4 + 4)

                    def ld(kt=kt, vt=vt, ch=ch, h=h, tiles=tiles):
                        P.op("sp", lambda e: e.dma_start(out=kt.ap, in_=kT_scr[h, :, ch * 2048:(ch + 1) * 2048]),
                             reads=[kbufs[t * H + h] for t in tiles], writes=[kt.buf], dsem=kt.dsem)
                        P.op("sp", lambda e: e.dma_start(
                            out=vt.ap, in_=v_scr[ch * 2048:(ch + 1) * 2048, h * 128:(h + 1) * 128].rearrange("(b p) d -> p b d", p=128)),
                            reads=[vbufs[t * 4 + j] for t in tiles for j in range(4)], writes=[vt.buf], dsem=vt.dsem)
                    first_of_chunk[len(G) + len(blocks)] = len(loads)
                    loads.append(ld)
                    for b_ in range(15, -1, -1):
                        gb = ch * 16 + b_
                        delta = gb * 128 - q0
                        m = masks[(delta + 128) // 128] if delta >= -128 else None
                        blocks.append((kt.ap[:, b_ * 128:(b_ + 1) * 128], vt.ap[:, b_, :], m, [kt.buf, vt.buf]))
            for i, (kap, vap, m, bufs) in enumerate(blocks):
                G.append((h, i, len(blocks), kap, vap, m, bufs))
        NG = len(G)

        def st0(g):
            h, i, B, kap, vap, m, bufs = G[g]
            s = S[g % 2]
            qh = qT.ap[:, h, :]
            for (lo, hi) in s.cg:
                P.op("pe", (lambda s=s, kap=kap, lo=lo, hi=hi, qh=qh: lambda e: e.matmul(s.ap[:, lo:hi], kap, qh[:, lo:hi], start=True, stop=True))(),
                     reads=bufs + [qT.buf], writes=[s.buf] if lo == 0 else [])
            s.buf.w = [P.ops["pe"][-1]]

        def st1a(g):
            s = S[g % 2]
            P.op("act", (lambda s=s: lambda e: e.activation(out=E.ap, in_=s.ap[:, 0:Wd], func=AF.Exp))(),
                 reads=[s.buf], writes=[E.buf])

        def st1b(g):
            h, i, B, kap, vap, m, bufs = G[g]
            sp = SP[g % 3]
            P.op("act", (lambda sp=sp: lambda e: e.activation(out=sp.ap, in_=E.ap, func=AF.Ln, bias=1.0))(),
                 reads=[E.buf], writes=[sp.buf])
            if m is not None:
                P.op("pool", (lambda sp=sp, m=m: lambda e: e.tensor_tensor(out=sp.ap, in0=sp.ap, in1=m, op=ALU.mult))(),
                     reads=[sp.buf] + CONST, writes=[sp.buf])

        def st2(g):
            h, i, B, kap, vap, m, bufs = G[g]
            s = S[g % 2]
            sp = SP[g % 3]
            tmp = TMP[g % 2]
            qh = qT.ap[:, h, :]
            for (lo, hi) in s.cg:
                def qk_tri(e, s=s, sp=sp, lo=lo, hi=hi, kap=kap, qh=qh):
                    e.matmul(s.ap[:, lo:hi], kap, qh[:, lo:hi], start=True, stop=False)
                    return e.matmul(s.ap[:, lo:hi], negtri, sp.ap[:, lo:hi], start=False, stop=True)
                P.op("pe", qk_tri, reads=[sp.buf, qT.buf] + bufs + CONST, writes=[s.buf] if lo == 0 else [])
            s.buf.w = [P.ops["pe"][-1]]
            if i < B - 1:
                for (lo, hi) in OB.cg:
                    P.op("pe", (lambda sp=sp, lo=lo, hi=hi: lambda e: e.matmul(OB.ap[:, lo:hi], ones_bf, sp.ap[:, lo:hi], start=True, stop=True))(),
                         reads=[sp.buf] + CONST, writes=[OB.buf] if lo == 0 else [])
                OB.buf.w = [P.ops["pe"][-1]]
            if i == 0:
                P.op("dve", (lambda s=s, tmp=tmp: lambda e: e.tensor_copy(out=tmp.ap, in_=s.ap[:, 0:Wd]))(),
                     reads=[s.buf], writes=[tmp.buf])
                if B > 1:
                    P.op("dve", lambda e: e.tensor_copy(out=RACC.ap, in_=OB.ap[:, 0:Wd]), reads=[OB.buf], writes=[RACC.buf])
            else:
                P.op("dve", (lambda s=s, tmp=tmp: lambda e: e.tensor_tensor(
                    out=tmp.ap, in0=s.ap[:, 0:Wd], in1=RACC.ap, op=ALU.subtract))(),
                    reads=[s.buf, RACC.buf], writes=[tmp.buf])
                if i < B - 1:
                    P.op("dve", lambda e: e.tensor_tensor(out=RACC.ap, in0=OB.ap[:, 0:Wd], in1=RACC.ap, op=ALU.add),
                         reads=[OB.buf, RACC.buf], writes=[RACC.buf])

        def st3a(g):
            h, i, B, kap, vap, m, bufs = G[g]
            tmp = TMP[g % 2]
            ww = WW[g % 4]
            P.op("act", (lambda tmp=tmp, ww=ww: lambda e: e.activation(out=ww.ap, in_=tmp.ap, func=AF.Exp))(),
                 reads=[tmp.buf], writes=[ww.buf])
            if m is not None:
                P.op("pool", (lambda ww=ww, m=m: lambda e: e.tensor_tensor(out=ww.ap, in0=ww.ap, in1=m, op=ALU.mult))(),
                     reads=[ww.buf] + CONST, writes=[ww.buf])

        def st3b(g):
            h, i, B, kap, vap, m, bufs = G[g]
            ww = WW[g % 4]
            for (lo, hi) in OUT.cg:
                P.op("pe", (lambda vap=vap, ww=ww, lo=lo, hi=hi, i=i, B=B: lambda e: e.matmul(
                    OUT.ap[:, lo:hi], vap, ww.ap[:, lo:hi], start=(i == 0), stop=(i == B - 1)))(),
                    reads=bufs + [ww.buf], writes=[OUT.buf] if (lo == 0 and i == 0) else [])
            OUT.buf.w = [P.ops["pe"][-1]]
            if i == B - 1:
                head_norm(h)

        def head_norm(h):
            P.op("act", lambda e: e.activation(out=sqa.ap, in_=OUT.ap[:, 0:Wd], func=AF.Square), reads=[OUT.buf], writes=[sqa.buf])
            P.op("act", lambda e: e.activation(out=osb.ap, in_=OUT.ap[:, 0:Wd], func=AF.Copy, scale=g_att_pp[:, h:h + 1]),
                 reads=[OUT.buf] + CONST, writes=[osb.buf])
            for (lo, hi) in OUT.cg:
                P.op("pe", (lambda lo=lo, hi=hi: lambda e: e.matmul(OUT.ap[:, lo:hi], ones_bf, sqa.ap[:, lo:hi], start=True, stop=True))(),
                     reads=[sqa.buf] + CONST, writes=[OUT.buf] if lo == 0 else [])
            OUT.buf.w = [P.ops["pe"][-1]]
            P.op("act", lambda e: e.activation(out=rstda.ap, in_=OUT.ap[:, 0:Wd], func=AF.Sqrt, scale=1.0 / 128, bias=EPS),
                 reads=[OUT.buf], writes=[rstda.buf])
            P.op("dve", lambda e: e.reciprocal(out=rstda.ap, in_=rstda.ap), reads=[rstda.buf], writes=[rstda.buf])
            P.op("dve", lambda e: e.tensor_tensor(out=mix.ap[:, h, :], in0=osb.ap, in1=rstda.ap, op=ALU.mult),
                 reads=[osb.buf, rstda.buf], writes=[mixh[h]])

        for ld in loads[0:2]:
            ld()
        for g in range(NG + 3):
            if g in first_of_chunk and first_of_chunk[g] >= 1 and first_of_chunk[g] + 1 < len(loads):
                loads[first_of_chunk[g] + 1]()
            if g < NG:
                st0(g)
                st1a(g)
                st1b(g)
            if g >= 3:
                st3b(g - 3)
            if 2 <= g <= NG + 1:
                st3a(g - 2)
            if 1 <= g <= NG:
                st2(g - 1)
        P.barrier()
        AR.reset(m_att)

        AR.limit = AR.cap - act_bytes
        h2T = T(AR.alloc([KC, Wd], BF16), "h2T")
        WR2 = [T(AR.alloc([KC, 256], BF16), "wr2_%d" % i, P.dsem("wr2_%d" % i)) for i in range(4)]
        wr2_i = [0]

        first = (si == 0 and not is_smp)
        tidA = [0]

        def load_w2(src):
            k = wr2_i[0] % 4
            t = WR2[k]
            wr2_i[0] += 1
            tid = tidA[0]
            tidA[0] += 1
            if first:
                P.op("pool", lambda e: e.dma_start(out=t.ap, in_=src.rearrange("(k p) c -> p k c", p=128)),
                     writes=[t.buf], dsem=t.dsem)
                P.op("sp", lambda e: e.dma_start(out=wsA[tid], in_=t.ap.rearrange("p a b -> p (a b)")),
                     reads=[t.buf], writes=[wsA_buf[tid]], dsem=P.dsem("wr2s_%d" % k))
            else:
                P.op("sp", lambda e: e.dma_start(out=t.ap.rearrange("p a b -> p (a b)"), in_=wsA[tid]),
                     reads=[wsA_buf[tid]], writes=[t.buf], dsem=t.dsem)
            return t
        PJ = [T(ps_f32(2 * i, 2), "pk%d" % i, psum=True) for i in range(4)]
        sq2 = T(AR.alloc([Wd], BF16), "sq2")
        rstd2 = T(AR.alloc([Wd], F32), "rstd2")
        gsb = T(AR.alloc([Wd], F32), "gsb")
        gcv = T(AR.alloc([Wd], F32), "gcv")
        xmid = []
        for cp in range(8):
            wt = load_w2(w_o[:, cp * 256:(cp + 1) * 256])
            for ci in range(2):
                c = cp * 2 + ci
                pj = PJ[c % 3]
                for (lo, hi) in CG:
                    P.op("pe", mm_group(pj.ap, [(wt.ap[:, kc, ci * 128:(ci + 1) * 128], mix.ap[:, kc, :]) for kc in range(KC)], lo, hi),
                         reads=[wt.buf] + mixh, writes=[pj.buf] if lo == 0 else [])
                pj.buf.w = [P.ops["pe"][-1]]
                xb = Buf("xmid")
                P.op("dve", (lambda c=c, pj=pj: lambda e: e.tensor_tensor(out=xT.ap[:, c, :], in0=pj.ap[:, 0:Wd], in1=xT.ap[:, c, :], op=ALU.add))(),
                     reads=[pj.buf] + xsub, writes=[xb])
                xmid.append(xb)
        nrm2 = PJ[3]
        sqs = []
        for c in range(KC):
            sqc = T(AR.alloc([Wd], BF16), "sqc") if c < 2 else sqs[c - 2]
            sqs.append(sqc)
        for c in range(KC):
            sqc = sqs[c]
            P.op("act", (lambda c=c, sqc=sqc: lambda e: e.activation(out=sqc.ap, in_=xT.ap[:, c, :], func=AF.Square))(),
                 reads=[xmid[c]], writes=[sqc.buf])
            for (lo, hi) in CG:
                P.op("pe", (lambda c=c, sqc=sqc, lo=lo, hi=hi: lambda e: e.matmul(nrm2.ap[:, lo:hi], ones_bf, sqc.ap[:, lo:hi],
                                                                               start=(c == 0), stop=(c == KC - 1)))(),
                     reads=[sqc.buf] + CONST, writes=[nrm2.buf] if (lo == 0 and c == 0) else [])
        nrm2.buf.w = [P.ops["pe"][-1]]
        P.op("act", lambda e: e.activation(out=rstd2.ap, in_=nrm2.ap[:, 0:Wd], func=AF.Sqrt, scale=1.0 / D, bias=EPS),
             reads=[nrm2.buf], writes=[rstd2.buf])
        P.op("dve", lambda e: e.reciprocal(out=rstd2.ap, in_=rstd2.ap), reads=[rstd2.buf], writes=[rstd2.buf])
        h2b = []
        for c in range(KC):
            hb = Buf("h2")
            P.op("dve", (lambda c=c: lambda e: e.scalar_tensor_tensor(out=h2T.ap[:, c, :], in0=xT.ap[:, c, :], scalar=g_ffn_pp[:, c:c + 1],
                                                                      in1=rstd2.ap, op0=ALU.mult, op1=ALU.mult))(),
                 reads=[xmid[c], rstd2.buf] + CONST, writes=[hb])
            h2b.append(hb)
        if DEBUG:
            P.op("sp", lambda e: e.dma_start(out=dbg_mix[si, :, 0:16 * Wd], in_=mix.ap.rearrange("p a b -> p (a b)")), reads=mixh, dsem=P.dsem("dbg0"))
            P.op("sp", lambda e: e.dma_start(out=dbg_xmid[si, :, 0:16 * Wd], in_=xT.ap.rearrange("p a b -> p (a b)")), reads=xmid, dsem=P.dsem("dbg1"))
            P.op("sp", lambda e: e.dma_start(out=dbg_h2[si, :, 0:16 * Wd], in_=h2T.ap.rearrange("p a b -> p (a b)")), reads=h2b, dsem=P.dsem("dbg2"))
        actb = []
        for fp in range(nf // 2):
            wg = load_w2(w_gu[:, fp * 256:(fp + 1) * 256])
            wu = load_w2(w_gu[:, dff + fp * 256:dff + (fp + 1) * 256])
            for fi in range(2):
                f = fp * 2 + fi
                pg, pu = PJ[(f % 2) * 2], PJ[(f % 2) * 2 + 1]
                for (lo, hi) in CG:
                    P.op("pe", mm_group(pg.ap, [(wg.ap[:, kc, fi * 128:(fi + 1) * 128], h2T.ap[:, kc, :]) for kc in range(KC)], lo, hi),
                         reads=[wg.buf] + h2b, writes=[pg.buf] if lo == 0 else [])
                pg.buf.w = [P.ops["pe"][-1]]
                P.op("pe", (lambda pu=pu, wu=wu, fi=fi: lambda e: [e.matmul(
                    pu.ap[:, 0:nown], wu.ap[:, kc, fi * 128:(fi + 1) * 128], h2T.ap[:, kc, HALO:Wd],
                    start=(kc == 0), stop=(kc == KC - 1)) for kc in range(KC)][-1])(),
                    reads=[wu.buf] + h2b, writes=[pu.buf])
                P.op("act", (lambda pg=pg: lambda e: e.activation(out=gsb.ap, in_=pg.ap[:, 0:Wd], func=AF.Copy))(),
                     reads=[pg.buf], writes=[gsb.buf])
                if is_smp:
                    P.op("dve", (lambda f=f: lambda e: e.tensor_copy(out=gsb.ap[:, 2:4], in_=stf.ap[:, f, :]))(),
                         reads=[stf.buf, gsb.buf], writes=[gsb.buf])
                P.op("dve", (lambda f=f: lambda e: e.tensor_copy(out=flast.ap[:, f * 2:f * 2 + 2], in_=gsb.ap[:, Wd - 2:Wd]))(),
                     reads=[gsb.buf], writes=[flast.buf])
                P.op("dve", (lambda f=f: lambda e: e.tensor_scalar(out=gcv.ap[:, HALO:Wd], in0=gsb.ap[:, HALO:Wd],
                                                                  scalar1=wffn_pp[:, f * 3 + 2:f * 3 + 3], scalar2=None, op0=ALU.mult))(),
                     reads=[gsb.buf] + CONST, writes=[gcv.buf])
                for tap in (1, 0):
                    P.op("dve", (lambda f=f, tap=tap: lambda e: e.scalar_tensor_tensor(
                        out=gcv.ap[:, HALO:Wd], in0=gsb.ap[:, HALO - 2 + tap:Wd - 2 + tap], scalar=wffn_pp[:, f * 3 + tap:f * 3 + tap + 1],
                        in1=gcv.ap[:, HALO:Wd], op0=ALU.mult, op1=ALU.add))(),
                        reads=[gsb.buf, gcv.buf] + CONST, writes=[gcv.buf])
                P.op("act", lambda e: e.activation(out=gcv.ap[:, HALO:Wd], in_=gcv.ap[:, HALO:Wd], func=AF.Silu),
                     reads=[gcv.buf], writes=[gcv.buf])
                ab = Buf("act")
                P.op("dve", (lambda f=f, pu=pu: lambda e: e.tensor_tensor(out=actT.ap[:, f, :], in0=pu.ap[:, 0:nown], in1=gcv.ap[:, HALO:Wd], op=ALU.mult))(),
                     reads=[pu.buf, gcv.buf], writes=[ab])
                actb.append(ab)
        P.op("sp", lambda e: e.dma_start(out=ffn_last[si], in_=flast.ap), reads=[flast.buf], dsem=P.dsem("flo"))
        P.barrier()
        AR.reset(m_low)
        WD = [T(AR.alloc([4, 512], BF16), "wd%d" % i, P.dsem("wd%d" % i)) for i in range(8)]
        wd_i = [0]
        ysb = [T(AR.alloc([D], F32), "ysb%d" % j) for j in range(len(osub))]
        xtk = [T(AR.alloc([512], F32), "xtk%d" % i, P.dsem("xtk%d" % i)) for i in range(2)]
        yo = [T(AR.alloc([D], F32), "yo%d" % i, P.dsem("yo%d" % i)) for i in range(1)]
        junk2 = T(AR.alloc([D], BF16), "junk2")
        ss2 = T(AR.alloc([8], F32), "ss2")
        rs2 = T(AR.alloc([8], F32), "rs2")
        ACC = [T(ps_f32(i), "acc%d" % i, psum=True) for i in range(8)]
        ydst = y_smp if is_smp else y_own[si]
        xi = 0
        nkt = KC // 4
        nft = nf // 4
        for cg in range(4):
            accs = [ACC[(cg % 2) * 4 + j] for j in range(len(osub))]
            ntile = nkt + nft
            for ti in range(ntile):
                kq = wd_i[0] % 8
                t = WD[kq]
                wd_i[0] += 1
                tid = cg * ntile + ti
                if ti < nkt:
                    src = w_o[ti * 512:(ti + 1) * 512, cg * 512:(cg + 1) * 512]
                else:
                    src = w_dn[(ti - nkt) * 512:(ti - nkt + 1) * 512, cg * 512:(cg + 1) * 512]
                if first:
                    P.op("pool", (lambda t=t, src=src: lambda e: e.dma_start(out=t.ap, in_=src.rearrange("(k p) c -> p k c", p=128)))(),
                         writes=[t.buf], dsem=t.dsem)
                    P.op("sp", (lambda t=t, tid=tid: lambda e: e.dma_start(out=wsB[tid], in_=t.ap.rearrange("p a b -> p (a b)")))(),
                         reads=[t.buf], writes=[wsB_buf[tid]], dsem=P.dsem("wds_%d" % kq))
                else:
                    P.op("sp", (lambda t=t, tid=tid: lambda e: e.dma_start(out=t.ap.rearrange("p a b -> p (a b)"), in_=wsB[tid]))(),
                         reads=[wsB_buf[tid]], writes=[t.buf], dsem=t.dsem)
                for j, (r0, nr) in enumerate(osub):
                    def dmm(e, t=t, ti=ti, j=j, r0=r0, nr=nr, acc=accs[j]):
                        ins = None
                        for k4 in range(4):
                            if ti < nkt:
                                l = mix.ap[:, ti * 4 + k4, HALO + r0:HALO + r0 + nr]
                            else:
                                l = actT.ap[:, (ti - nkt) * 4 + k4, r0:r0 + nr]
                            ins = e.matmul(acc.ap[0:nr, :], l, t.ap[:, k4, :], start=(ti == 0 and k4 == 0),
                                           stop=(ti == ntile - 1 and k4 == 3))
                        return ins
                    P.op("pe", dmm, reads=[t.buf] + (mixh if ti < nkt else actb), writes=[accs[j].buf] if ti == 0 else [])
                    accs[j].buf.w = [P.ops["pe"][-1]]
            for j, (r0, nr) in enumerate(osub):
                xt = xtk[xi % 2]
                xi += 1
                P.op("sp", (lambda xt=xt, r0=r0, nr=nr, cg=cg: lambda e: e.dma_start(
                    out=xt.ap[0:nr, :], in_=xsrc[HALO + r0:HALO + r0 + nr, cg * 512:(cg + 1) * 512]))(),
                    writes=[xt.buf], dsem=xt.dsem)
                P.op("dve", (lambda xt=xt, j=j, nr=nr, cg=cg, acc=accs[j]: lambda e: e.tensor_tensor(
                    out=ysb[j].ap[0:nr, cg * 512:(cg + 1) * 512], in0=acc.ap[0:nr, :], in1=xt.ap[0:nr, :], op=ALU.add))(),
                    reads=[accs[j].buf, xt.buf], writes=[ysb[j].buf] if cg == 0 else [])
                if cg > 0:
                    ysb[j].buf.w = [P.ops["dve"][-1]]
        for j, (r0, nr) in enumerate(osub):
            y = yo[0]
            P.op("dve", (lambda nr=nr: lambda e: e.memset(ss2.ap[0:nr, 0:1], 0.0))(), writes=[ss2.buf])
            P.op("act", (lambda j=j, nr=nr: lambda e: e.activation(out=junk2.ap[0:nr, :], in_=ysb[j].ap[0:nr, :], func=AF.Square,
                                                                  accum_out=ss2.ap[0:nr, 0:1]))(),
                 reads=[ysb[j].buf, ss2.buf], writes=[junk2.buf, ss2.buf])
            P.op("act", (lambda nr=nr: lambda e: e.activation(out=rs2.ap[0:nr, 0:1], in_=ss2.ap[0:nr, 0:1], func=AF.Sqrt, scale=1.0 / D, bias=EPS))(),
                 reads=[ss2.buf], writes=[rs2.buf])
            P.op("dve", (lambda nr=nr: lambda e: e.reciprocal(out=rs2.ap[0:nr, 0:1], in_=rs2.ap[0:nr, 0:1]))(), reads=[rs2.buf], writes=[rs2.buf])
            P.op("dve", (lambda j=j, nr=nr, y=y: lambda e: e.scalar_tensor_tensor(out=y.ap[0:nr, :], in0=ysb[j].ap[0:nr, :], scalar=rs2.ap[0:nr, 0:1],
                                                                               in1=g_fin_rep[0:nr, :], op0=ALU.mult, op1=ALU.mult))(),
                 reads=[ysb[j].buf, rs2.buf] + CONST, writes=[y.buf])
            P.op("sp", (lambda y=y, r0=r0, nr=nr: lambda e: e.dma_start(out=ydst[r0:r0 + nr, :], in_=y.ap[0:nr, :]))(),
                 reads=[y.buf], dsem=y.dsem)
        P.barrier()
        AR.limit = AR.cap
        AR.reset(m_slot)

    kbufs, vbufs = phase_A()
    for si in range(NSLOT):
        slot(si, False, kbufs, vbufs)
    slot(NSLOT, True, kbufs, vbufs)

    P.finalize()
    with nc.Block() as block:
        @block.tensor
        def _(e):
            P.emit("pe", e)

        @block.scalar
        def _(e):
            P.emit("act", e)

        @block.vector
        def _(e):
            P.emit("dve", e)

        @block.gpsimd
        def _(e):
            P.emit("pool", e)

        @block.sync
        def _(e):
            P.emit("sp", e)
            P.final_waits(e)
    stack.close()
    return nc


def make_consts(nf, w_conv, g_att_out, g_conv_out, g_mix, g_ffn, w_ffn_conv, g_final):
    W = 512 + HALO
    WS = DEC + HALO
    ident = np.eye(128, dtype=np.float32)
    ones = np.ones((128, 128), np.float32)
    j = np.arange(128)[:, None]
    s = np.arange(128)[None, :]
    negtri = np.where(j >= s, -1.0, 0.0).astype(np.float32)
    r = np.arange(128)[:, None]
    col = np.arange(W)[None, :]
    ms = []
    for mi in range(5):
        delta = (mi - 1) * 128
        ms.append((col > delta + r + HALO).astype(np.float32))
    cs = np.arange(WS)[None, :]
    msk_s = ((cs > r + HALO) & (r < DEC)).astype(np.float32)
    c_bf = np.concatenate([ident, ones, negtri] + ms + [msk_s], axis=1).astype(np.float32)
    pp = lambda v, n: np.ascontiguousarray(v.reshape(n, 128).T)
    parts = [ident,
             np.broadcast_to(g_mix.reshape(1, D), (128, D)),
             np.broadcast_to(g_final.reshape(1, D), (128, D)),
             pp(g_ffn.reshape(-1), KC),
             np.ascontiguousarray(g_att_out.reshape(8, 128).T),
             np.ascontiguousarray(g_conv_out.reshape(8, 128).T),
             np.ascontiguousarray(w_conv.reshape(3, 8, 128).transpose(2, 1, 0)).reshape(128, 24),
             np.ascontiguousarray(w_ffn_conv.reshape(3, nf, 128).transpose(2, 1, 0)).reshape(128, 3 * nf)]
    c_f32 = np.concatenate([np.asarray(p, np.float32) for p in parts], axis=1)
    return np.ascontiguousarray(c_bf), np.ascontiguousarray(c_f32)


_NC_CACHE = {}


def run(inputs, nslot, debug=False):
    f = lambda a: np.ascontiguousarray(np.asarray(a, dtype=np.float32))
    xp = f(inputs["x_prompt"])[0]
    xsm = f(inputs["x_sample"])
    ck = f(inputs["cache_k"])[0]
    cv = f(inputs["cache_v"])[0]
    stc = f(inputs["state_conv"])[0]
    stf = f(inputs["state_ffn_conv"])[0]
    w_in = f(inputs["w_in"])[0]
    w_o = f(inputs["w_o"])[0]
    w_gu = f(inputs["w_gate_up"])[0]
    w_dn = f(inputs["w_down"])[0]
    dff = w_dn.shape[0]
    nf = dff // 128
    seq = xp.shape[0]
    NT = 8 * nslot
    assert seq == NT * 512
    W = 512 + HALO
    WS = DEC + HALO
    c_bf, c_f32 = make_consts(nf, f(inputs["w_conv"])[0], f(inputs["g_att_out"])[0], f(inputs["g_conv_out"])[0],
                              f(inputs["g_mix"])[0], f(inputs["g_ffn"])[0], f(inputs["w_ffn_conv"])[0], f(inputs["g_final"]))
    key = (nslot, nf, debug)
    if key not in _NC_CACHE:
        _NC_CACHE[key] = build({"nslot": nslot, "nf": nf, "debug": debug})
    nc = _NC_CACHE[key]
    in_maps = []
    for c in range(NCORE):
        npad = (7 - c) * 512
        x_seq = np.zeros((NT * 512, D), np.float32)
        x_seq[npad:] = xp[:NT * 512 - npad]
        x_own = np.zeros((nslot, W, D), np.float32)
        for s in range(nslot):
            q0 = (8 * s + c) * 512
            lo = q0 - HALO
            if lo < 0:
                x_own[s, -lo:] = xp[0:q0 + 512]
            else:
                x_own[s] = xp[lo:q0 + 512]
        x_smp = np.zeros((WS, D), np.float32)
        x_smp[HALO:] = xsm[c]
        in_maps.append({
            "x_seq": x_seq, "x_own": x_own, "x_smp": x_smp,
            "cache_k": np.ascontiguousarray(ck[c]), "cache_v": np.ascontiguousarray(cv[c]),
            "st_conv": np.ascontiguousarray(stc[c].reshape(2, 8, 128).transpose(2, 1, 0)),
            "st_ffn": np.ascontiguousarray(stf[c].reshape(2, nf, 128).transpose(2, 1, 0)),
            "w_in": w_in, "w_o": w_o, "w_gu": w_gu, "w_dn": w_dn, "c_bf": c_bf, "c_f32": c_f32,
        })
    res = run_bass_kernel_spmd(nc, in_maps, core_ids=list(range(NCORE)))
    R = res.results
    if debug:
        run.dbg = [{k: np.asarray(r[k]) for k in ("dbg_mix", "dbg_xmid", "dbg_h2", "dbg_att", "dbg_nrm")} for r in R]
    y_p = np.zeros((1, seq, D), np.float32)
    k_p = np.zeros((1, 1, seq, H, HD), np.float32)
    v_p = np.zeros((1, 1, seq, H, HD), np.float32)
    y_s = np.zeros((NCORE, DEC, D), np.float32)
    k_s = np.zeros((1, NCORE, DEC, H, HD), np.float32)
    v_s = np.zeros((1, NCORE, DEC, H, HD), np.float32)
    c_s = np.zeros((1, NCORE, 2, 1024), np.float32)
    f_s = np.zeros((1, NCORE, 2, dff), np.float32)
    for c in range(NCORE):
        r = R[c]
        for s in range(nslot):
            t = 8 * s + c
            y_p[0, t * 512:(t + 1) * 512] = np.asarray(r["y_own"])[s]
            k_p[0, 0, t * 512:(t + 1) * 512] = np.asarray(r["k_own"])[s].reshape(512, H, HD)
            v_p[0, 0, t * 512:(t + 1) * 512] = np.asarray(r["v_own"])[s].reshape(512, H, HD)
        y_s[c] = np.asarray(r["y_smp"])
        k_s[0, c] = np.asarray(r["k_smp"]).reshape(DEC, H, HD)
        v_s[0, c] = np.asarray(r["v_smp"]).reshape(DEC, H, HD)
        cl = np.asarray(r["conv_last"])
        fl = np.asarray(r["ffn_last"])
        c_s[0, c] = cl[nslot].reshape(128, 8, 2).transpose(2, 1, 0).reshape(2, 1024)
        f_s[0, c] = fl[nslot].reshape(128, nf, 2).transpose(2, 1, 0).reshape(2, dff)
        if c == NCORE - 1:
            c_p = cl[nslot - 1].reshape(128, 8, 2).transpose(2, 1, 0).reshape(1, 1, 2, 1024).copy()
            f_p = fl[nslot - 1].reshape(128, nf, 2).transpose(2, 1, 0).reshape(1, 1, 2, dff).copy()
    return (y_p, y_s, k_p, v_p, c_p, f_p, k_s, v_s, c_s, f_s)


def kernel(**inputs):
    return run(inputs, 4)
```

```python
import numpy as np
from contextlib import ExitStack
import concourse.bass as bass
import concourse.mybir as mybir
from concourse.bass_utils import run_bass_kernel_spmd

F32 = mybir.dt.float32
BF16 = mybir.dt.bfloat16
AF = mybir.ActivationFunctionType
ALU = mybir.AluOpType

D = 2048
KC = 16
H = 8
HD = 128
DFF = 5632
NF = 44
EPS = 1e-6
SB_SCALE = HD ** -0.5
NCORE = 8
PAST = 1024
DEC = 32
HALO = 4
SAME_ENGINE_SYNC = True


class DSem:
    def __init__(self, sem):
        self.sem = sem
        self.count = 0


class Op:
    __slots__ = ("eng", "fn", "deps", "signals", "value", "sem", "is_dma")


class Buf:
    __slots__ = ("w", "r", "name", "psum")

    def __init__(self, name="", psum=False):
        self.w = []
        self.r = {}
        self.name = name
        self.psum = psum


class Prog:
    ENGS = ("pe", "act", "dve", "pool", "sp")

    def __init__(self, nc, stack):
        self.nc = nc
        self.stack = stack
        self.ops = {e: [] for e in self.ENGS}
        self.esem = {e: DSem(stack.enter_context(nc.semaphore("sem_" + e))) for e in ("pe", "act", "dve", "pool")}
        self.bar_deps = []
        self.bar_seen = set(self.ENGS)
        self.last = {}
        self.dma_open = []
        self.dsems = {}

    def dsem(self, name):
        if name not in self.dsems:
            self.dsems[name] = DSem(self.stack.enter_context(self.nc.semaphore("d_" + name)))
        return self.dsems[name]

    def op(self, eng, fn, reads=(), writes=(), dsem=None):
        o = Op()
        o.eng = eng
        o.fn = fn
        o.is_dma = dsem is not None
        o.signals = False
        o.value = 0
        o.sem = None
        deps = set()
        for b in reads:
            deps.update(b.w)
            if b.psum:
                for en, lst in b.r.items():
                    if en != eng:
                        deps.update(lst)
        for b in writes:
            deps.update(b.w)
            for lst in b.r.values():
                deps.update(lst)
        if eng not in self.bar_seen:
            deps.update(self.bar_deps)
            self.bar_seen.add(eng)
        o.deps = deps
        for b in reads:
            if o.is_dma:
                b.r.setdefault("dma", []).append(o)
            else:
                b.r[eng] = [o]
        for b in writes:
            b.w = [o]
            b.r = {}
        if dsem is not None:
            dsem.count += 16
            o.sem = dsem
            o.value = dsem.count
            o.signals = True
            self.dma_open.append(o)
        else:
            self.last[eng] = o
        self.ops[eng].append(o)
        return o

    def barrier(self):
        self.bar_deps = list(self.last.values()) + list(self.dma_open)
        self.dma_open = []
        self.bar_seen = set()

    @staticmethod
    def needs_wait(o, d):
        if d.is_dma:
            return True
        if d.eng == o.eng:
            if o.is_dma:
                return True
            if o.eng == "pe":
                return False
            return SAME_ENGINE_SYNC
        return True

    def finalize(self):
        for e in self.ENGS:
            for o in self.ops[e]:
                for d in o.deps:
                    if self.needs_wait(o, d):
                        d.signals = True
        for e in ("pe", "act", "dve", "pool"):
            cnt = 0
            for o in self.ops[e]:
                if o.is_dma:
                    continue
                if o.signals:
                    cnt += 1
                    o.value = cnt
                    o.sem = self.esem[e]

    def emit(self, eng_name, e):
        waited = {}
        for o in self.ops[eng_name]:
            need = {}
            for d in o.deps:
                if not self.needs_wait(o, d):
                    continue
                k = id(d.sem)
                if need.get(k, (None, 0))[1] < d.value:
                    need[k] = (d.sem, d.value)
            for k, (ds, val) in need.items():
                if waited.get(k, 0) < val:
                    e.wait_ge(ds.sem, val)
                    waited[k] = val
            ins = o.fn(e)
            if o.signals:
                ins.then_inc(o.sem.sem, 16 if o.is_dma else 1)

    def final_waits(self, e):
        seen = {}
        for en in self.ENGS:
            for o in self.ops[en]:
                if o.is_dma:
                    k = id(o.sem)
                    if seen.get(k, (None, 0))[1] < o.value:
                        seen[k] = (o.sem, o.value)
        for k, (ds, val) in seen.items():
            e.wait_ge(ds.sem, val)


class Arena:
    def __init__(self, nc, nbytes):
        self.h = nc.alloc_sbuf_tensor("arena", [128, nbytes // 4], F32)
        self.ap = self.h.ap()
        self.off = 0
        self.cap = nbytes
        self.limit = nbytes

    def alloc(self, free_shape, dtype, top=False):
        esz = 4 if dtype == F32 else 2
        n = int(np.prod(free_shape))
        nb = (n * esz + 31) // 32 * 32
        if top:
            off = self.cap - nb
        else:
            assert self.off + nb <= self.limit, "SBUF arena overflow: need %d have %d" % (self.off + nb, self.limit)
            off = self.off
            self.off += nb
        a = self.ap[:, off // 4:(off + nb) // 4]
        if dtype != F32:
            a = a.bitcast(dtype)
        a = a[:, 0:n]
        if len(free_shape) == 2:
            a = a.rearrange("p (a b) -> p a b", a=free_shape[0])
        elif len(free_shape) == 3:
            a = a.rearrange("p (a b c) -> p a b c", a=free_shape[0], b=free_shape[1])
        return a

    def mark(self):
        return self.off

    def reset(self, m):
        self.off = m


class T:
    def __init__(self, ap, name="", dsem=None, psum=False):
        self.cg = None
        self.ap = ap
        self.buf = Buf(name, psum)
        self.dsem = dsem


def col_groups(w):
    if w <= 512:
        return [(0, w)]
    return [(0, 512), (512, w)]


def build(cfg):
    NSLOT = cfg["nslot"]
    NT = 8 * NSLOT
    NTOK = NT * 512
    W = 512 + HALO
    WS = DEC + HALO
    nf = cfg.get("nf", NF)
    dff = nf * 128

    nc = bass.Bass("TRN2", target_bir_lowering=False)
    stack = ExitStack()

    def din(name, shape, dt=F32):
        return nc.dram_tensor(name, list(shape), dt, kind="ExternalInput").ap()

    def dout(name, shape, dt=F32):
        return nc.dram_tensor(name, list(shape), dt, kind="ExternalOutput").ap()

    x_seq = din("x_seq", [NTOK, D])
    x_own = din("x_own", [NSLOT, W, D])
    x_smp = din("x_smp", [WS, D])
    cache_k = din("cache_k", [PAST, H, HD])
    cache_v = din("cache_v", [PAST, H, HD])
    st_conv = din("st_conv", [128, 8, 2])
    st_ffn = din("st_ffn", [128, nf, 2])
    w_in = din("w_in", [D, 6 * 1024])
    w_o = din("w_o", [D, D])
    w_gu = din("w_gu", [D, 2 * dff])
    w_dn = din("w_dn", [dff, D])
    c_bf = din("c_bf", [128, 3 * 128 + 5 * W + WS])
    c_f32 = din("c_f32", [128, 128 + 2 * D + KC + 8 + 8 + 24 + 3 * nf])

    y_own = dout("y_own", [NSLOT, 512, D])
    y_smp = dout("y_smp", [DEC, D])
    k_own = dout("k_own", [NSLOT, 512, 1024])
    v_own = dout("v_own", [NSLOT, 512, 1024])
    k_smp = dout("k_smp", [DEC, 1024])
    v_smp = dout("v_smp", [DEC, 1024])
    conv_last = dout("conv_last", [NSLOT + 1, 128, 16])
    ffn_last = dout("ffn_last", [NSLOT + 1, 128, nf * 2])

    DEBUG = cfg.get("debug", False)
    if DEBUG:
        dbg_mix = dout("dbg_mix", [NSLOT + 1, 128, 16 * W], BF16)
        dbg_xmid = dout("dbg_xmid", [NSLOT + 1, 128, 16 * W], F32)
        dbg_h2 = dout("dbg_h2", [NSLOT + 1, 128, 16 * W], BF16)
        dbg_att = dout("dbg_att", [NSLOT + 1, 128, 8 * W], F32)
        dbg_nrm = dout("dbg_nrm", [NSLOT + 1, 128, 8 * W], F32)
    kT_scr = nc.dram_tensor("kT_scr", [H, 128, NTOK], BF16).ap()
    v_scr = nc.dram_tensor("v_scr", [NTOK, 1024], BF16).ap()

    NTA = 8 + nf
    NTB = 4 * (KC // 4 + nf // 4)
    wsA = nc.dram_tensor("wsA", [NTA, 128, KC * 256], BF16).ap()
    wsB = nc.dram_tensor("wsB", [NTB, 128, 4 * 512], BF16).ap()
    wsA_buf = [Buf("wsA%d" % i) for i in range(NTA)]
    wsB_buf = [Buf("wsB%d" % i) for i in range(NTB)]
    P = Prog(nc, stack)
    AR = Arena(nc, 207 * 1024)
    PSH = nc.alloc_psum_tensor("psum_all", [128, 4096], F32)
    PS = PSH.ap()

    def ps_f32(bank, nb=1):
        return PS[:, bank * 512:(bank + nb) * 512]

    def ps_bf(bank, nb=1):
        return PS[:, bank * 512:(bank + nb) * 512].bitcast(BF16)

    cb = T(AR.alloc([3 * 128 + 5 * W + WS], BF16), "cb", P.dsem("cb"))
    cf = T(AR.alloc([128 + 2 * D + KC + 8 + 8 + 24 + 3 * nf], F32), "cf", P.dsem("cf"))
    ident_bf = cb.ap[:, 0:128]
    ones_bf = cb.ap[:, 128:256]
    negtri = cb.ap[:, 256:384]
    masks = [cb.ap[:, 384 + i * W:384 + (i + 1) * W] for i in range(5)]
    mask_s = cb.ap[:, 384 + 5 * W:384 + 5 * W + WS]
    o = 0
    ident_f = cf.ap[:, o:o + 128]; o += 128
    g_mix_rep = cf.ap[:, o:o + D]; o += D
    g_fin_rep = cf.ap[:, o:o + D]; o += D
    g_ffn_pp = cf.ap[:, o:o + KC]; o += KC
    g_att_pp = cf.ap[:, o:o + 8]; o += 8
    g_conv_pp = cf.ap[:, o:o + 8]; o += 8
    wconv_pp = cf.ap[:, o:o + 24]; o += 24
    wffn_pp = cf.ap[:, o:o + 3 * nf]; o += 3 * nf

    P.op("pool", lambda e: e.dma_start(out=cb.ap, in_=c_bf), writes=[cb.buf], dsem=cb.dsem)
    P.op("sp", lambda e: e.dma_start(out=cf.ap, in_=c_f32), writes=[cf.buf], dsem=cf.dsem)
    CONST = [cb.buf, cf.buf]

    mixT = [None]
    persist_mark = AR.mark()

    def rmsnorm_rows(xs, rows, xn, tmp, g_rep):
        junk, ss, rs = tmp
        P.op("dve", lambda e: e.memset(ss.ap[0:rows, 0:1], 0.0), writes=[ss.buf])
        P.op("act", lambda e: e.activation(out=junk.ap[0:rows, :], in_=xs.ap[0:rows, :], func=AF.Square,
                                           accum_out=ss.ap[0:rows, 0:1]),
             reads=[xs.buf], writes=[junk.buf, ss.buf])
        P.op("act", lambda e: e.activation(out=rs.ap[0:rows, 0:1], in_=ss.ap[0:rows, 0:1], func=AF.Sqrt,
                                           scale=1.0 / D, bias=EPS),
             reads=[ss.buf], writes=[rs.buf])
        P.op("dve", lambda e: e.reciprocal(out=rs.ap[0:rows, 0:1], in_=rs.ap[0:rows, 0:1]),
             reads=[rs.buf], writes=[rs.buf])
        P.op("dve", lambda e: e.scalar_tensor_tensor(out=xn.ap[0:rows, :], in0=xs.ap[0:rows, :],
                                                     scalar=rs.ap[0:rows, 0:1], in1=g_rep[0:rows, :],
                                                     op0=ALU.mult, op1=ALU.mult),
             reads=[xs.buf, rs.buf] + CONST, writes=[xn.buf])

    def mm_group(out_ap, pairs, lo, hi):
        def fn(e):
            ins = None
            n = len(pairs)
            for i, (l, r) in enumerate(pairs):
                ins = e.matmul(out_ap[:, lo:hi], l, r[:, lo:hi], start=(i == 0), stop=(i == n - 1))
            return ins
        return fn

    def phase_A():
        m0 = AR.mark()
        wk = T(AR.alloc([KC, 1024], BF16), "wk", P.dsem("wk"))
        wv = T(AR.alloc([KC, 1024], BF16), "wv", P.dsem("wv"))
        for (wt, c0) in ((wk, 1024), (wv, 2048)):
            wops = []
            for half in range(2):
                src = w_in[:, c0 + half * 512:c0 + (half + 1) * 512].rearrange("(k p) c -> p k c", p=128)
                wops.append(P.op("pool", (lambda wt=wt, half=half, src=src:
                                          lambda e: e.dma_start(out=wt.ap[:, :, half * 512:(half + 1) * 512], in_=src))(),
                                 writes=[], dsem=wt.dsem))
            wt.buf.w = [wops[-1]]
        XS = [T(AR.alloc([D], F32), "xs%d" % i, P.dsem("xs%d" % i)) for i in range(3)]
        XN = [T(AR.alloc([D], BF16), "xn%d" % i) for i in range(2)]
        junk = T(AR.alloc([D], BF16), "junk")
        ss = T(AR.alloc([8], F32), "ss")
        rs = T(AR.alloc([8], F32), "rs")
        HT = [T(AR.alloc([KC, 512], BF16), "hT%d" % i) for i in range(2)]
        HTB = [[Buf("htb") for j in range(4)] for i in range(2)]
        KST = [[T(AR.alloc([512], BF16), "kst", P.dsem("kst%d_%d" % (i, h))) for h in range(H)] for i in range(2)]
        VST = [[T(AR.alloc([1024], BF16), "vst", P.dsem("vst%d_%d" % (i, j))) for j in range(4)] for i in range(2)]
        VSTH = [[[Buf("vsth") for cg in range(2)] for j in range(4)] for i in range(2)]
        TP = [T(ps_bf(2 * i, 2), "tp%d" % i, psum=True) for i in range(2)]
        KP = [T(ps_f32(4 + i), "kp%d" % i, psum=True) for i in range(2)]
        VP = [T(ps_f32(6 + i), "vp%d" % i, psum=True) for i in range(2)]
        kbufs, vbufs = [None] * (NT * H), [None] * (NT * 4)
        ev = [0]

        def prepA(i, j):
            n = i * 4 + j
            xs, xn = XS[n % 3], XN[n % 2]
            P.op("sp", lambda e: e.dma_start(out=xs.ap, in_=x_seq[n * 128:(n + 1) * 128, :]), writes=[xs.buf], dsem=xs.dsem)
            rmsnorm_rows(xs, 128, xn, (junk, ss, rs), g_mix_rep)

        def prepB(i, j):
            n = i * 4 + j
            xn, tp = XN[n % 2], TP[n % 2]
            ht = HT[i % 2]

            def tr(e):
                ins = None
                for kc in range(KC):
                    ins = e.transpose(tp.ap[:, kc * 128:(kc + 1) * 128], xn.ap[:, kc * 128:(kc + 1) * 128], ident_bf)
                return ins
            P.op("pe", tr, reads=[xn.buf] + CONST, writes=[tp.buf])
            P.op("act", lambda e: e.activation(out=ht.ap[:, :, j * 128:(j + 1) * 128],
                                               in_=tp.ap.rearrange("p (k t) -> p k t", k=KC), func=AF.Copy),
                 reads=[tp.buf], writes=[HTB[i % 2][j]])

        def evac(dst_ap, src, wbuf):
            if ev[0] % 2 == 0:
                P.op("dve", lambda e: e.tensor_copy(out=dst_ap, in_=src.ap), reads=[src.buf], writes=[wbuf])
            else:
                P.op("act", lambda e: e.activation(out=dst_ap, in_=src.ap, func=AF.Copy), reads=[src.buf], writes=[wbuf])
            ev[0] += 1

        def kgroup(i, h):
            ht = HT[i % 2]
            kp = KP[h % 2]
            P.op("pe", lambda e: [e.matmul(kp.ap, wk.ap[:, kc, h * 128:(h + 1) * 128], ht.ap[:, kc, :],
                                           start=(kc == 0), stop=(kc == KC - 1)) for kc in range(KC)][-1],
                 reads=[wk.buf] + HTB[i % 2], writes=[kp.buf])
            kst = KST[i % 2][h]
            evac(kst.ap, kp, kst.buf)
            kb = Buf("kscr")
            P.op("pool", lambda e: e.dma_start(out=kT_scr[h, :, i * 512:(i + 1) * 512], in_=kst.ap),
                 reads=[kst.buf], writes=[kb], dsem=kst.dsem)
            kbufs[i * H + h] = kb

        def vgroup(i, j, cg):
            ht = HT[i % 2]
            vp = VP[cg]
            P.op("pe", lambda e: [e.matmul(vp.ap, ht.ap[:, kc, j * 128:(j + 1) * 128], wv.ap[:, kc, cg * 512:(cg + 1) * 512],
                                           start=(kc == 0), stop=(kc == KC - 1)) for kc in range(KC)][-1],
                 reads=[wv.buf, HTB[i % 2][j]], writes=[vp.buf])
            vst = VST[i % 2][j]
            evac(vst.ap[:, cg * 512:(cg + 1) * 512], vp, VSTH[i % 2][j][cg])
            if cg == 1:
                vb = Buf("vscr")
                P.op("pool", lambda e: e.dma_start(out=v_scr[i * 512 + j * 128:i * 512 + (j + 1) * 128, :], in_=vst.ap),
                     reads=VSTH[i % 2][j], writes=[vb], dsem=vst.dsem)
                vbufs[i * 4 + j] = vb

        subs = [(i, j) for i in range(NT) for j in range(4)]
        prepA(0, 0)
        for n in range(4):
            prepB(*subs[n])
            prepA(*subs[n + 1])
        nxt = 4
        for i in range(NT):
            groups = [("k", h) for h in range(H)] + [("v", j, cg) for j in range(4) for cg in range(2)]
            for gi, g in enumerate(groups):
                if g[0] == "k":
                    kgroup(i, g[1])
                else:
                    vgroup(i, g[1], g[2])
                if gi % 4 == 1 and nxt < len(subs):
                    prepB(*subs[nxt])
                    if nxt + 1 < len(subs):
                        prepA(*subs[nxt + 1])
                    nxt += 1
        P.barrier()
        AR.reset(m0)
        return kbufs, vbufs

    def slot(si, is_smp, kbufs, vbufs):
        Wd = WS if is_smp else W
        CG = col_groups(Wd)
        nown = DEC if is_smp else 512
        xsrc = x_smp if is_smp else x_own[si]
        subt = [(r0, min(128, Wd - r0)) for r0 in range(0, Wd, 128)]
        osub = [(r0, min(128, nown - r0)) for r0 in range(0, nown, 128)]
        m_slot = AR.mark()
        mix = T(AR.alloc([16, Wd], BF16), "mixT")
        mixh = [Buf("mixh%d" % i) for i in range(16)]
        qT = T(AR.alloc([H, Wd], BF16), "qT")
        clast = T(AR.alloc([16], F32), "clast")
        flast = T(AR.alloc([nf * 2], F32), "flast")
        stc = T(AR.alloc([8, 2], F32), "stc", P.dsem("stc"))
        stf = T(AR.alloc([nf, 2], F32), "stf", P.dsem("stf"))
        kTs = T(AR.alloc([H, 128], BF16), "kTs")
        vs = T(AR.alloc([H, 128], BF16), "vs")
        m_low = AR.mark()
        xT = T(AR.alloc([KC, Wd], F32), "xT")
        m_x = AR.mark()
        hTo = T(AR.alloc([KC, Wd], BF16), "hTo")
        actT = T(AR.alloc([nf, nown], BF16, top=True), "actT")
        act_bytes = (nf * nown * 2 + 31) // 32 * 32
        if is_smp:
            P.op("sp", lambda e: e.dma_start(out=stc.ap, in_=st_conv), writes=[stc.buf], dsem=stc.dsem)
            P.op("sp", lambda e: e.dma_start(out=stf.ap, in_=st_ffn), writes=[stf.buf], dsem=stf.dsem)
        m_s1 = AR.mark()

        XS = [T(AR.alloc([D], F32), "xso%d" % i, P.dsem("xso%d" % i)) for i in range(2)]
        XN = [T(AR.alloc([D], BF16), "xno%d" % i) for i in range(2)]
        junk = T(AR.alloc([D], BF16), "junk")
        ss = T(AR.alloc([8], F32), "ss")
        rs = T(AR.alloc([8], F32), "rs")
        TPb = [T(ps_bf(2 * i, 2), "tpb%d" % i, psum=True) for i in range(2)]
        TPf = [T(ps_f32(4 + 2 * i, 2), "tpf%d" % i, psum=True) for i in range(2)]
        hsub, xsub = [], []
        for n, (r0, nr) in enumerate(subt):
            xs, xn, tpb = XS[n % 2], XN[n % 2], TPb[n % 2]
            P.op("sp", (lambda xs=xs, r0=r0, nr=nr: lambda e: e.dma_start(out=xs.ap[0:nr, :], in_=xsrc[r0:r0 + nr, :]))(),
                 writes=[xs.buf], dsem=xs.dsem)
            rmsnorm_rows(xs, nr, xn, (junk, ss, rs), g_mix_rep)

            def trb(e, xn=xn, tpb=tpb, nr=nr):
                ins = None
                for kc in range(KC):
                    ins = e.transpose(tpb.ap[:, kc * 128:kc * 128 + nr], xn.ap[0:nr, kc * 128:(kc + 1) * 128],
                                      ident_bf[0:nr, 0:nr])
                return ins
            P.op("pe", trb, reads=[xn.buf] + CONST, writes=[tpb.buf])
            hb = Buf("hsub")
            P.op("act", (lambda tpb=tpb, r0=r0, nr=nr: lambda e: e.activation(
                out=hTo.ap[:, :, r0:r0 + nr], in_=tpb.ap.rearrange("p (k t) -> p k t", k=KC)[:, :, 0:nr], func=AF.Copy))(),
                reads=[tpb.buf], writes=[hb])
            hsub.append(hb)
            for half in range(2):
                tpf = TPf[half]

                def trf(e, xs=xs, tpf=tpf, nr=nr, half=half):
                    ins = None
                    for k8 in range(8):
                        kc = half * 8 + k8
                        ins = e.transpose(tpf.ap[:, k8 * 128:k8 * 128 + nr], xs.ap[0:nr, kc * 128:(kc + 1) * 128],
                                          ident_f[0:nr, 0:nr])
                    return ins
                P.op("pe", trf, reads=[xs.buf] + CONST, writes=[tpf.buf])
                xb = Buf("xsub")
                P.op("dve", (lambda tpf=tpf, r0=r0, nr=nr, half=half: lambda e: e.tensor_copy(
                    out=xT.ap[:, half * 8:(half + 1) * 8, r0:r0 + nr],
                    in_=tpf.ap.rearrange("p (k t) -> p k t", k=8)[:, :, 0:nr]))(),
                    reads=[tpf.buf], writes=[xb])
                xsub.append(xb)
        P.barrier()
        AR.reset(m_s1)

        WR = [T(AR.alloc([KC, 256], BF16), "wr%d" % i, P.dsem("wr%d" % i)) for i in range(6)]
        wr_i = [0]

        def load_w(src_cols):
            t = WR[wr_i[0] % 6]
            wr_i[0] += 1
            P.op("pool", (lambda t=t, src=src_cols: lambda e: e.dma_start(out=t.ap, in_=src.rearrange("(k p) c -> p k c", p=128)))(),
                 writes=[t.buf], dsem=t.dsem)
            return t
        PJ = [T(ps_f32(2 * i, 2), "pj%d" % i, psum=True) for i in range(4)]
        u_sb = T(AR.alloc([Wd], F32), "u_sb")
        cu = T(AR.alloc([Wd], F32), "cu")
        ycv = T(AR.alloc([Wd], F32), "ycv")
        sq = T(AR.alloc([Wd], BF16), "sq")
        rstd = T(AR.alloc([Wd], F32), "rstd")

        def proj(pj, wt, c0):
            for (lo, hi) in CG:
                P.op("pe", mm_group(pj.ap, [(wt.ap[:, kc, c0:c0 + 128], hTo.ap[:, kc, :]) for kc in range(KC)], lo, hi),
                     reads=[wt.buf] + hsub, writes=[pj.buf] if lo == 0 else [])
            pj.buf.w = [P.ops["pe"][-1]]

        def group_norm_to_mix(src, lo_c, gcol, mrow, mbuf):
            nrm = PJ[3]
            P.op("act", lambda e: e.activation(out=sq.ap[:, lo_c:Wd], in_=src.ap[:, lo_c:Wd], func=AF.Square),
                 reads=[src.buf], writes=[sq.buf])
            for (lo, hi) in CG:
                l2 = max(lo, lo_c)
                P.op("pe", (lambda l2=l2, hi=hi: lambda e: e.matmul(nrm.ap[:, l2:hi], ones_bf, sq.ap[:, l2:hi], start=True, stop=True))(),
                     reads=[sq.buf] + CONST, writes=[nrm.buf] if lo == 0 else [])
            nrm.buf.w = [P.ops["pe"][-1]]
            P.op("act", lambda e: e.activation(out=rstd.ap[:, lo_c:Wd], in_=nrm.ap[:, lo_c:Wd], func=AF.Sqrt,
                                               scale=1.0 / 128, bias=EPS), reads=[nrm.buf], writes=[rstd.buf])
            P.op("dve", lambda e: e.reciprocal(out=rstd.ap[:, lo_c:Wd], in_=rstd.ap[:, lo_c:Wd]),
                 reads=[rstd.buf], writes=[rstd.buf])
            P.op("dve", lambda e: e.scalar_tensor_tensor(out=mix.ap[:, mrow, lo_c:Wd], in0=src.ap[:, lo_c:Wd], scalar=gcol,
                                                         in1=rstd.ap[:, lo_c:Wd], op0=ALU.mult, op1=ALU.mult),
                 reads=[src.buf, rstd.buf] + CONST, writes=[mbuf])

        for gp in range(4):
            wC = load_w(w_in[:, 4096 + gp * 256:4096 + (gp + 1) * 256])
            wU = load_w(w_in[:, 5120 + gp * 256:5120 + (gp + 1) * 256])
            wB = load_w(w_in[:, 3072 + gp * 256:3072 + (gp + 1) * 256])
            for gi in range(2):
                g = gp * 2 + gi
                pC, pU, pB = PJ[0], PJ[1], PJ[2]
                proj(pC, wC, gi * 128)
                proj(pU, wU, gi * 128)
                proj(pB, wB, gi * 128)
                P.op("act", lambda e: e.activation(out=u_sb.ap, in_=pU.ap[:, 0:Wd], func=AF.Copy),
                     reads=[pU.buf], writes=[u_sb.buf])
                P.op("dve", lambda e: e.tensor_tensor(out=cu.ap, in0=pC.ap[:, 0:Wd], in1=u_sb.ap, op=ALU.mult),
                     reads=[pC.buf, u_sb.buf], writes=[cu.buf])
                if is_smp:
                    P.op("dve", (lambda g=g: lambda e: e.tensor_copy(out=cu.ap[:, 2:4], in_=stc.ap[:, g, :]))(),
                         reads=[stc.buf, cu.buf], writes=[cu.buf])
                P.op("dve", (lambda g=g: lambda e: e.tensor_copy(out=clast.ap[:, g * 2:g * 2 + 2], in_=cu.ap[:, Wd - 2:Wd]))(),
                     reads=[cu.buf], writes=[clast.buf])
                P.op("dve", (lambda g=g: lambda e: e.tensor_scalar(out=ycv.ap[:, 2:Wd], in0=cu.ap[:, 2:Wd],
                                                                  scalar1=wconv_pp[:, g * 3 + 2:g * 3 + 3], scalar2=None,
                                                                  op0=ALU.mult))(),
                     reads=[cu.buf] + CONST, writes=[ycv.buf])
                for tap in (1, 0):
                    P.op("dve", (lambda g=g, tap=tap: lambda e: e.scalar_tensor_tensor(
                        out=ycv.ap[:, 2:Wd], in0=cu.ap[:, tap:Wd - 2 + tap], scalar=wconv_pp[:, g * 3 + tap:g * 3 + tap + 1],
                        in1=ycv.ap[:, 2:Wd], op0=ALU.mult, op1=ALU.add))(),
                        reads=[cu.buf, ycv.buf] + CONST, writes=[ycv.buf])
                P.op("dve", lambda e: e.tensor_tensor(out=ycv.ap[:, 2:Wd], in0=pB.ap[:, 2:Wd], in1=ycv.ap[:, 2:Wd], op=ALU.mult),
                     reads=[pB.buf, ycv.buf], writes=[ycv.buf])
                group_norm_to_mix(ycv, 2, g_conv_pp[:, g:g + 1], 8 + g, mixh[8 + g])
        P.op("sp", lambda e: e.dma_start(out=conv_last[si], in_=clast.ap), reads=[clast.buf], dsem=P.dsem("clo"))
        for hp in range(4):
            wQ = load_w(w_in[:, hp * 256:(hp + 1) * 256])
            for hi_ in range(2):
                h = hp * 2 + hi_
                pq = PJ[h % 3]
                proj(pq, wQ, hi_ * 128)
                P.op("act", (lambda h=h, pq=pq: lambda e: e.activation(out=qT.ap[:, h, :], in_=pq.ap[:, 0:Wd], func=AF.Copy, scale=SB_SCALE))(),
                     reads=[pq.buf], writes=[qT.buf] if h == 0 else [])
                if h > 0:
                    qT.buf.w.append(P.ops["act"][-1])
        kvst = [T(AR.alloc([256], F32), "kvst%d" % i, P.dsem("kvst%d" % i)) for i in range(2)]
        if is_smp:
            P.op("pool", lambda e: e.memset(kTs.ap, 0.0), writes=[kTs.buf])
            P.op("pool", lambda e: e.memset(vs.ap, 0.0), writes=[vs.buf])
        kvi = 0
        for (which, c0, dst) in (("k", 1024, k_smp if is_smp else k_own[si]), ("v", 2048, v_smp if is_smp else v_own[si])):
            for q4 in range(4):
                wt = load_w(w_in[:, c0 + q4 * 256:c0 + (q4 + 1) * 256])
                for (r0, nr) in osub:
                    pj = PJ[kvi % 3]
                    st = kvst[kvi % 2]
                    kvi += 1
                    P.op("pe", (lambda pj=pj, wt=wt, r0=r0, nr=nr: lambda e: [e.matmul(
                        pj.ap[0:nr, 0:256], hTo.ap[:, kc, HALO + r0:HALO + r0 + nr], wt.ap[:, kc, :],
                        start=(kc == 0), stop=(kc == KC - 1)) for kc in range(KC)][-1])(),
                        reads=[wt.buf] + hsub, writes=[pj.buf])
                    P.op("act", (lambda pj=pj, st=st, nr=nr: lambda e: e.activation(out=st.ap[0:nr, :], in_=pj.ap[0:nr, 0:256], func=AF.Copy))(),
                         reads=[pj.buf], writes=[st.buf])
                    if is_smp and which == "v":
                        P.op("dve", (lambda st=st, q4=q4, nr=nr: lambda e: e.tensor_copy(
                            out=vs.ap[0:nr, 2 * q4:2 * q4 + 2, :], in_=st.ap[0:nr, :].rearrange("p (a b) -> p a b", a=2)))(),
                            reads=[st.buf, vs.buf], writes=[vs.buf])
                    P.op("sp", (lambda st=st, r0=r0, nr=nr, q4=q4, dst=dst: lambda e: e.dma_start(
                        out=dst[r0:r0 + nr, q4 * 256:(q4 + 1) * 256], in_=st.ap[0:nr, :]))(),
                        reads=[st.buf], dsem=st.dsem)
                if is_smp and which == "k":
                    for hi_ in range(2):
                        h = q4 * 2 + hi_
                        pj = PJ[3]
                        P.op("pe", (lambda pj=pj, wt=wt, hi_=hi_: lambda e: [e.matmul(
                            pj.ap[:, 0:DEC], wt.ap[:, kc, hi_ * 128:(hi_ + 1) * 128], hTo.ap[:, kc, HALO:HALO + DEC],
                            start=(kc == 0), stop=(kc == KC - 1)) for kc in range(KC)][-1])(),
                            reads=[wt.buf] + hsub, writes=[pj.buf])
                        P.op("dve", (lambda pj=pj, h=h: lambda e: e.tensor_copy(out=kTs.ap[:, h, 0:DEC], in_=pj.ap[:, 0:DEC]))(),
                             reads=[pj.buf, kTs.buf], writes=[kTs.buf])
        P.barrier()

        m_att = m_x
        AR.reset(m_x)
        E = T(AR.alloc([Wd], F32), "E")
        SP = [T(AR.alloc([Wd], BF16), "sp%d" % i) for i in range(3)]
        TMP = [T(AR.alloc([Wd], F32), "tmp%d" % i) for i in range(2)]
        WW = [T(AR.alloc([Wd], BF16), "ww%d" % i) for i in range(4)]
        RACC = T(AR.alloc([Wd], F32), "racc")
        def ptile(k, name):
            t = T(ps_f32(2 * k, 2), name, psum=True)
            t.cg = CG
            return t
        S = [ptile(i, "S%d" % i) for i in range(2)]
        OB = ptile(2, "OB")
        OUT = ptile(3, "OUT")
        sqa = T(AR.alloc([Wd], BF16), "sqa")
        rstda = T(AR.alloc([Wd], F32), "rstda")
        osb = T(AR.alloc([Wd], F32), "osb")
        if is_smp:
            ck = T(AR.alloc([8, H * 128], BF16), "ck", P.dsem("ck"))
            cv = T(AR.alloc([8, H * 128], BF16), "cv", P.dsem("cv"))
            kTc = T(AR.alloc([H, PAST], BF16), "kTc")
            P.op("pool", lambda e: e.dma_start(out=ck.ap, in_=cache_k.rearrange("(b p) h d -> p b (h d)", p=128)),
                 writes=[ck.buf], dsem=ck.dsem)
            P.op("pool", lambda e: e.dma_start(out=cv.ap, in_=cache_v.rearrange("(b p) h d -> p b (h d)", p=128)),
                 writes=[cv.buf], dsem=cv.dsem)
            tpk = T(ps_bf(0, 1), "tpk", psum=True)
            for h in range(H):
                def trk(e, h=h):
                    ins = None
                    for b in range(8):
                        ins = e.transpose(tpk.ap[:, b * 128:(b + 1) * 128], ck.ap[:, b, h * 128:(h + 1) * 128], ident_bf)
                    return ins
                P.op("pe", trk, reads=[ck.buf] + CONST, writes=[tpk.buf])
                P.op("dve", (lambda h=h: lambda e: e.tensor_copy(out=kTc.ap[:, h, :], in_=tpk.ap))(),
                     reads=[tpk.buf], writes=[kTc.buf] if h == 0 else [])
                if h > 0:
                    kTc.buf.w.append(P.ops["dve"][-1])
            P.barrier()
            KCH = VCH = None
        else:
            KCH = [T(AR.alloc([2048], BF16), "kch%d" % i, P.dsem("kch%d" % i)) for i in range(3)]
            VCH = [T(AR.alloc([16, 128], BF16), "vch%d" % i, P.dsem("vch%d" % i)) for i in range(3)]
        chunk_i = [0]
        G = []
        loads = []
        first_of_chunk = {}
        for h in range(H):
            blocks = []
            if is_smp:
                blocks.append((kTs.ap[:, h, :], vs.ap[:, h, :], mask_s, [kTs.buf, vs.buf]))
                for b_ in range(7, -1, -1):
                    blocks.append((kTc.ap[:, h, b_ * 128:(b_ + 1) * 128], cv.ap[:, b_, h * 128:(h + 1) * 128], None,
                                   [kTc.buf, cv.buf]))
            else:
                NK = 4096 * (si + 1)
                q0 = NK - 512
                for ch in range(NK // 2048 - 1, -1, -1):
                    kt = KCH[chunk_i[0] % 3]
                    vt = VCH[chunk_i[0] % 3]
                    chunk_i[0] += 1
                    tiles = range(ch * 4, ch * 4 + 4)

                    def ld(kt=kt, vt=vt, ch=ch, h=h, tiles=tiles):
                        P.op("sp", lambda e: e.dma_start(out=kt.ap, in_=kT_scr[h, :, ch * 2048:(ch + 1) * 2048]),
                             reads=[kbufs[t * H + h] for t in tiles], writes=[kt.buf], dsem=kt.dsem)
                        P.op("sp", lambda e: e.dma_start(
                            out=vt.ap, in_=v_scr[ch * 2048:(ch + 1) * 2048, h * 128:(h + 1) * 128].rearrange("(b p) d -> p b d", p=128)),
                            reads=[vbufs[t * 4 + j] for t in tiles for j in range(4)], writes=[vt.buf], dsem=vt.dsem)
                    first_of_chunk[len(G) + len(blocks)] = len(loads)
                    loads.append(ld)
                    for b_ in range(15, -1, -1):
                        gb = ch * 16 + b_
                        delta = gb * 128 - q0
                        m = masks[(delta + 128) // 128] if delta >= -128 else None
                        blocks.append((kt.ap[:, b_ * 128:(b_ + 1) * 128], vt.ap[:, b_, :], m, [kt.buf, vt.buf]))
            for i, (kap, vap, m, bufs) in enumerate(blocks):
                G.append((h, i, len(blocks), kap, vap, m, bufs))
        NG = len(G)

        def st0(g):
            h, i, B, kap, vap, m, bufs = G[g]
            s = S[g % 2]
            qh = qT.ap[:, h, :]
            for (lo, hi) in s.cg:
                P.op("pe", (lambda s=s, kap=kap, lo=lo, hi=hi, qh=qh: lambda e: e.matmul(s.ap[:, lo:hi], kap, qh[:, lo:hi], start=True, stop=True))(),
                     reads=bufs + [qT.buf], writes=[s.buf] if lo == 0 else [])
            s.buf.w = [P.ops["pe"][-1]]

        def st1a(g):
            s = S[g % 2]
            P.op("act", (lambda s=s: lambda e: e.activation(out=E.ap, in_=s.ap[:, 0:Wd], func=AF.Exp))(),
                 reads=[s.buf], writes=[E.buf])

        def st1b(g):
            h, i, B, kap, vap, m, bufs = G[g]
            sp = SP[g % 3]
            P.op("act", (lambda sp=sp: lambda e: e.activation(out=sp.ap, in_=E.ap, func=AF.Ln, bias=1.0))(),
                 reads=[E.buf], writes=[sp.buf])
            if m is not None:
                P.op("pool", (lambda sp=sp, m=m: lambda e: e.tensor_tensor(out=sp.ap, in0=sp.ap, in1=m, op=ALU.mult))(),
                     reads=[sp.buf] + CONST, writes=[sp.buf])

        def st2(g):
            h, i, B, kap, vap, m, bufs = G[g]
            s = S[g % 2]
            sp = SP[g % 3]
            tmp = TMP[g % 2]
            qh = qT.ap[:, h, :]
            for (lo, hi) in s.cg:
                def qk_tri(e, s=s, sp=sp, lo=lo, hi=hi, kap=kap, qh=qh):
                    e.matmul(s.ap[:, lo:hi], kap, qh[:, lo:hi], start=True, stop=False)
                    return e.matmul(s.ap[:, lo:hi], negtri, sp.ap[:, lo:hi], start=False, stop=True)
                P.op("pe", qk_tri, reads=[sp.buf, qT.buf] + bufs + CONST, writes=[s.buf] if lo == 0 else [])
            s.buf.w = [P.ops["pe"][-1]]
            if i < B - 1:
                for (lo, hi) in OB.cg:
                    P.op("pe", (lambda sp=sp, lo=lo, hi=hi: lambda e: e.matmul(OB.ap[:, lo:hi], ones_bf, sp.ap[:, lo:hi], start=True, stop=True))(),
                         reads=[sp.buf] + CONST, writes=[OB.buf] if lo == 0 else [])
                OB.buf.w = [P.ops["pe"][-1]]
            if i == 0:
                P.op("dve", (lambda s=s, tmp=tmp: lambda e: e.tensor_copy(out=tmp.ap, in_=s.ap[:, 0:Wd]))(),
                     reads=[s.buf], writes=[tmp.buf])
                if B > 1:
                    P.op("dve", lambda e: e.tensor_copy(out=RACC.ap, in_=OB.ap[:, 0:Wd]), reads=[OB.buf], writes=[RACC.buf])
            else:
                P.op("dve", (lambda s=s, tmp=tmp: lambda e: e.tensor_tensor(
                    out=tmp.ap, in0=s.ap[:, 0:Wd], in1=RACC.ap, op=ALU.subtract))(),
                    reads=[s.buf, RACC.buf], writes=[tmp.buf])
                if i < B - 1:
                    P.op("dve", lambda e: e.tensor_tensor(out=RACC.ap, in0=OB.ap[:, 0:Wd], in1=RACC.ap, op=ALU.add),
                         reads=[OB.buf, RACC.buf], writes=[RACC.buf])

        def st3a(g):
            h, i, B, kap, vap, m, bufs = G[g]
            tmp = TMP[g % 2]
            ww = WW[g % 4]
            P.op("act", (lambda tmp=tmp, ww=ww: lambda e: e.activation(out=ww.ap, in_=tmp.ap, func=AF.Exp))(),
                 reads=[tmp.buf], writes=[ww.buf])
            if m is not None:
                P.op("pool", (lambda ww=ww, m=m: lambda e: e.tensor_tensor(out=ww.ap, in0=ww.ap, in1=m, op=ALU.mult))(),
                     reads=[ww.buf] + CONST, writes=[ww.buf])

        def st3b(g):
            h, i, B, kap, vap, m, bufs = G[g]
            ww = WW[g % 4]
            for (lo, hi) in OUT.cg:
                P.op("pe", (lambda vap=vap, ww=ww, lo=lo, hi=hi, i=i, B=B: lambda e: e.matmul(
                    OUT.ap[:, lo:hi], vap, ww.ap[:, lo:hi], start=(i == 0), stop=(i == B - 1)))(),
                    reads=bufs + [ww.buf], writes=[OUT.buf] if (lo == 0 and i == 0) else [])
            OUT.buf.w = [P.ops["pe"][-1]]
            if i == B - 1:
                head_norm(h)

        def head_norm(h):
            P.op("act", lambda e: e.activation(out=sqa.ap, in_=OUT.ap[:, 0:Wd], func=AF.Square), reads=[OUT.buf], writes=[sqa.buf])
            P.op("act", lambda e: e.activation(out=osb.ap, in_=OUT.ap[:, 0:Wd], func=AF.Copy, scale=g_att_pp[:, h:h + 1]),
                 reads=[OUT.buf] + CONST, writes=[osb.buf])
            for (lo, hi) in OUT.cg:
                P.op("pe", (lambda lo=lo, hi=hi: lambda e: e.matmul(OUT.ap[:, lo:hi], ones_bf, sqa.ap[:, lo:hi], start=True, stop=True))(),
                     reads=[sqa.buf] + CONST, writes=[OUT.buf] if lo == 0 else [])
            OUT.buf.w = [P.ops["pe"][-1]]
            P.op("act", lambda e: e.activation(out=rstda.ap, in_=OUT.ap[:, 0:Wd], func=AF.Sqrt, scale=1.0 / 128, bias=EPS),
                 reads=[OUT.buf], writes=[rstda.buf])
            P.op("dve", lambda e: e.reciprocal(out=rstda.ap, in_=rstda.ap), reads=[rstda.buf], writes=[rstda.buf])
            P.op("dve", lambda e: e.tensor_tensor(out=mix.ap[:, h, :], in0=osb.ap, in1=rstda.ap, op=ALU.mult),
                 reads=[osb.buf, rstda.buf], writes=[mixh[h]])

        for ld in loads[0:2]:
            ld()
        for g in range(NG + 3):
            if g in first_of_chunk and first_of_chunk[g] >= 1 and first_of_chunk[g] + 1 < len(loads):
                loads[first_of_chunk[g] + 1]()
            if g < NG:
                st0(g)
                st1a(g)
                st1b(g)
            if 1 <= g <= NG:
                st2(g - 1)
            if 2 <= g <= NG + 1:
                st3a(g - 2)
            if g >= 3:
                st3b(g - 3)
        P.barrier()
        AR.reset(m_att)

        AR.limit = AR.cap - act_bytes
        h2T = T(AR.alloc([KC, Wd], BF16), "h2T")
        WR2 = [T(AR.alloc([KC, 256], BF16), "wr2_%d" % i, P.dsem("wr2_%d" % i)) for i in range(4)]
        wr2_i = [0]

        first = (si == 0 and not is_smp)
        tidA = [0]

        def load_w2(src):
            k = wr2_i[0] % 4
            t = WR2[k]
            wr2_i[0] += 1
            tid = tidA[0]
            tidA[0] += 1
            if first:
                P.op("pool", lambda e: e.dma_start(out=t.ap, in_=src.rearrange("(k p) c -> p k c", p=128)),
                     writes=[t.buf], dsem=t.dsem)
                P.op("sp", lambda e: e.dma_start(out=wsA[tid], in_=t.ap.rearrange("p a b -> p (a b)")),
                     reads=[t.buf], writes=[wsA_buf[tid]], dsem=P.dsem("wr2s_%d" % k))
            else:
                P.op("sp", lambda e: e.dma_start(out=t.ap.rearrange("p a b -> p (a b)"), in_=wsA[tid]),
                     reads=[wsA_buf[tid]], writes=[t.buf], dsem=P.dsem("wr2h_%d" % k))
            return t
        PJ = [T(ps_f32(2 * i, 2), "pk%d" % i, psum=True) for i in range(4)]
        sq2 = T(AR.alloc([Wd], BF16), "sq2")
        rstd2 = T(AR.alloc([Wd], F32), "rstd2")
        gsb = T(AR.alloc([Wd], F32), "gsb")
        gcv = T(AR.alloc([Wd], F32), "gcv")
        xmid = []
        for cp in range(8):
            wt = load_w2(w_o[:, cp * 256:(cp + 1) * 256])
            for ci in range(2):
                c = cp * 2 + ci
                pj = PJ[c % 3]
                for (lo, hi) in CG:
                    P.op("pe", mm_group(pj.ap, [(wt.ap[:, kc, ci * 128:(ci + 1) * 128], mix.ap[:, kc, :]) for kc in range(KC)], lo, hi),
                         reads=[wt.buf] + mixh, writes=[pj.buf] if lo == 0 else [])
                pj.buf.w = [P.ops["pe"][-1]]
                xb = Buf("xmid")
                P.op("dve", (lambda c=c, pj=pj: lambda e: e.tensor_tensor(out=xT.ap[:, c, :], in0=pj.ap[:, 0:Wd], in1=xT.ap[:, c, :], op=ALU.add))(),
                     reads=[pj.buf] + xsub, writes=[xb])
                xmid.append(xb)
        nrm2 = PJ[3]
        sqs = []
        for c in range(KC):
            sqc = T(AR.alloc([Wd], BF16), "sqc") if c < 2 else sqs[c - 2]
            sqs.append(sqc)
        for c in range(KC):
            sqc = sqs[c]
            P.op("act", (lambda c=c, sqc=sqc: lambda e: e.activation(out=sqc.ap, in_=xT.ap[:, c, :], func=AF.Square))(),
                 reads=[xmid[c]], writes=[sqc.buf])
            for (lo, hi) in CG:
                P.op("pe", (lambda c=c, sqc=sqc, lo=lo, hi=hi: lambda e: e.matmul(nrm2.ap[:, lo:hi], ones_bf, sqc.ap[:, lo:hi],
                                                                               start=(c == 0), stop=(c == KC - 1)))(),
                     reads=[sqc.buf] + CONST, writes=[nrm2.buf] if (lo == 0 and c == 0) else [])
        nrm2.buf.w = [P.ops["pe"][-1]]
        P.op("act", lambda e: e.activation(out=rstd2.ap, in_=nrm2.ap[:, 0:Wd], func=AF.Sqrt, scale=1.0 / D, bias=EPS),
             reads=[nrm2.buf], writes=[rstd2.buf])
        P.op("dve", lambda e: e.reciprocal(out=rstd2.ap, in_=rstd2.ap), reads=[rstd2.buf], writes=[rstd2.buf])
        h2b = []
        for c in range(KC):
            hb = Buf("h2")
            P.op("dve", (lambda c=c: lambda e: e.scalar_tensor_tensor(out=h2T.ap[:, c, :], in0=xT.ap[:, c, :], scalar=g_ffn_pp[:, c:c + 1],
                                                                      in1=rstd2.ap, op0=ALU.mult, op1=ALU.mult))(),
                 reads=[xmid[c], rstd2.buf] + CONST, writes=[hb])
            h2b.append(hb)
        if DEBUG:
            P.op("sp", lambda e: e.dma_start(out=dbg_mix[si, :, 0:16 * Wd], in_=mix.ap.rearrange("p a b -> p (a b)")), reads=mixh, dsem=P.dsem("dbg0"))
            P.op("sp", lambda e: e.dma_start(out=dbg_xmid[si, :, 0:16 * Wd], in_=xT.ap.rearrange("p a b -> p (a b)")), reads=xmid, dsem=P.dsem("dbg1"))
            P.op("sp", lambda e: e.dma_start(out=dbg_h2[si, :, 0:16 * Wd], in_=h2T.ap.rearrange("p a b -> p (a b)")), reads=h2b, dsem=P.dsem("dbg2"))
        actb = []
        for fp in range(nf // 2):
            wg = load_w2(w_gu[:, fp * 256:(fp + 1) * 256])
            wu = load_w2(w_gu[:, dff + fp * 256:dff + (fp + 1) * 256])
            for fi in range(2):
                f = fp * 2 + fi
                pg, pu = PJ[(f % 2) * 2], PJ[(f % 2) * 2 + 1]
                for (lo, hi) in CG:
                    P.op("pe", mm_group(pg.ap, [(wg.ap[:, kc, fi * 128:(fi + 1) * 128], h2T.ap[:, kc, :]) for kc in range(KC)], lo, hi),
                         reads=[wg.buf] + h2b, writes=[pg.buf] if lo == 0 else [])
                pg.buf.w = [P.ops["pe"][-1]]
                P.op("pe", (lambda pu=pu, wu=wu, fi=fi: lambda e: [e.matmul(
                    pu.ap[:, 0:nown], wu.ap[:, kc, fi * 128:(fi + 1) * 128], h2T.ap[:, kc, HALO:Wd],
                    start=(kc == 0), stop=(kc == KC - 1)) for kc in range(KC)][-1])(),
                    reads=[wu.buf] + h2b, writes=[pu.buf])
                P.op("act", (lambda pg=pg: lambda e: e.activation(out=gsb.ap, in_=pg.ap[:, 0:Wd], func=AF.Copy))(),
                     reads=[pg.buf], writes=[gsb.buf])
                if is_smp:
                    P.op("dve", (lambda f=f: lambda e: e.tensor_copy(out=gsb.ap[:, 2:4], in_=stf.ap[:, f, :]))(),
                         reads=[stf.buf, gsb.buf], writes=[gsb.buf])
                P.op("dve", (lambda f=f: lambda e: e.tensor_copy(out=flast.ap[:, f * 2:f * 2 + 2], in_=gsb.ap[:, Wd - 2:Wd]))(),
                     reads=[gsb.buf], writes=[flast.buf])
                P.op("dve", (lambda f=f: lambda e: e.tensor_scalar(out=gcv.ap[:, HALO:Wd], in0=gsb.ap[:, HALO:Wd],
                                                                  scalar1=wffn_pp[:, f * 3 + 2:f * 3 + 3], scalar2=None, op0=ALU.mult))(),
                     reads=[gsb.buf] + CONST, writes=[gcv.buf])
                for tap in (1, 0):
                    P.op("dve", (lambda f=f, tap=tap: lambda e: e.scalar_tensor_tensor(
                        out=gcv.ap[:, HALO:Wd], in0=gsb.ap[:, HALO - 2 + tap:Wd - 2 + tap], scalar=wffn_pp[:, f * 3 + tap:f * 3 + tap + 1],
                        in1=gcv.ap[:, HALO:Wd], op0=ALU.mult, op1=ALU.add))(),
                        reads=[gsb.buf, gcv.buf] + CONST, writes=[gcv.buf])
                P.op("act", lambda e: e.activation(out=gcv.ap[:, HALO:Wd], in_=gcv.ap[:, HALO:Wd], func=AF.Silu),
                     reads=[gcv.buf], writes=[gcv.buf])
                ab = Buf("act")
                P.op("dve", (lambda f=f, pu=pu: lambda e: e.tensor_tensor(out=actT.ap[:, f, :], in0=pu.ap[:, 0:nown], in1=gcv.ap[:, HALO:Wd], op=ALU.mult))(),
                     reads=[pu.buf, gcv.buf], writes=[ab])
                actb.append(ab)
        P.op("sp", lambda e: e.dma_start(out=ffn_last[si], in_=flast.ap), reads=[flast.buf], dsem=P.dsem("flo"))
        P.barrier()
        AR.reset(m_low)
        WD = [T(AR.alloc([4, 512], BF16), "wd%d" % i, P.dsem("wd%d" % i)) for i in range(8)]
        wd_i = [0]
        ysb = [T(AR.alloc([D], F32), "ysb%d" % j) for j in range(len(osub))]
        xtk = [T(AR.alloc([512], F32), "xtk%d" % i, P.dsem("xtk%d" % i)) for i in range(2)]
        yo = [T(AR.alloc([D], F32), "yo%d" % i, P.dsem("yo%d" % i)) for i in range(1)]
        junk2 = T(AR.alloc([D], BF16), "junk2")
        ss2 = T(AR.alloc([8], F32), "ss2")
        rs2 = T(AR.alloc([8], F32), "rs2")
        ACC = [T(ps_f32(i), "acc%d" % i, psum=True) for i in range(8)]
        ydst = y_smp if is_smp else y_own[si]
        xi = 0
        nkt = KC // 4
        nft = nf // 4
        for cg in range(4):
            accs = [ACC[(cg % 2) * 4 + j] for j in range(len(osub))]
            ntile = nkt + nft
            for ti in range(ntile):
                kq = wd_i[0] % 8
                t = WD[kq]
                wd_i[0] += 1
                tid = cg * ntile + ti
                if ti < nkt:
                    src = w_o[ti * 512:(ti + 1) * 512, cg * 512:(cg + 1) * 512]
                else:
                    src = w_dn[(ti - nkt) * 512:(ti - nkt + 1) * 512, cg * 512:(cg + 1) * 512]
                if first:
                    P.op("pool", (lambda t=t, src=src: lambda e: e.dma_start(out=t.ap, in_=src.rearrange("(k p) c -> p k c", p=128)))(),
                         writes=[t.buf], dsem=t.dsem)
                    P.op("sp", (lambda t=t, tid=tid: lambda e: e.dma_start(out=wsB[tid], in_=t.ap.rearrange("p a b -> p (a b)")))(),
                         reads=[t.buf], writes=[wsB_buf[tid]], dsem=P.dsem("wds_%d" % kq))
                else:
                    P.op("sp", (lambda t=t, tid=tid: lambda e: e.dma_start(out=t.ap.rearrange("p a b -> p (a b)"), in_=wsB[tid]))(),
                         reads=[wsB_buf[tid]], writes=[t.buf], dsem=P.dsem("wdh_%d" % kq))
                for j, (r0, nr) in enumerate(osub):
                    def dmm(e, t=t, ti=ti, j=j, r0=r0, nr=nr, acc=accs[j]):
                        ins = None
                        for k4 in range(4):
                            if ti < nkt:
                                l = mix.ap[:, ti * 4 + k4, HALO + r0:HALO + r0 + nr]
                            else:
                                l = actT.ap[:, (ti - nkt) * 4 + k4, r0:r0 + nr]
                            ins = e.matmul(acc.ap[0:nr, :], l, t.ap[:, k4, :], start=(ti == 0 and k4 == 0),
                                           stop=(ti == ntile - 1 and k4 == 3))
                        return ins
                    P.op("pe", dmm, reads=[t.buf] + (mixh if ti < nkt else actb), writes=[accs[j].buf] if ti == 0 else [])
                    accs[j].buf.w = [P.ops["pe"][-1]]
            for j, (r0, nr) in enumerate(osub):
                xt = xtk[xi % 2]
                xi += 1
                P.op("sp", (lambda xt=xt, r0=r0, nr=nr, cg=cg: lambda e: e.dma_start(
                    out=xt.ap[0:nr, :], in_=xsrc[HALO + r0:HALO + r0 + nr, cg * 512:(cg + 1) * 512]))(),
                    writes=[xt.buf], dsem=xt.dsem)
                P.op("dve", (lambda xt=xt, j=j, nr=nr, cg=cg, acc=accs[j]: lambda e: e.tensor_tensor(
                    out=ysb[j].ap[0:nr, cg * 512:(cg + 1) * 512], in0=acc.ap[0:nr, :], in1=xt.ap[0:nr, :], op=ALU.add))(),
                    reads=[accs[j].buf, xt.buf], writes=[ysb[j].buf] if cg == 0 else [])
                if cg > 0:
                    ysb[j].buf.w = [P.ops["dve"][-1]]
        for j, (r0, nr) in enumerate(osub):
            y = yo[0]
            P.op("dve", (lambda nr=nr: lambda e: e.memset(ss2.ap[0:nr, 0:1], 0.0))(), writes=[ss2.buf])
            P.op("act", (lambda j=j, nr=nr: lambda e: e.activation(out=junk2.ap[0:nr, :], in_=ysb[j].ap[0:nr, :], func=AF.Square,
                                                                  accum_out=ss2.ap[0:nr, 0:1]))(),
                 reads=[ysb[j].buf, ss2.buf], writes=[junk2.buf, ss2.buf])
            P.op("act", (lambda nr=nr: lambda e: e.activation(out=rs2.ap[0:nr, 0:1], in_=ss2.ap[0:nr, 0:1], func=AF.Sqrt, scale=1.0 / D, bias=EPS))(),
                 reads=[ss2.buf], writes=[rs2.buf])
            P.op("dve", (lambda nr=nr: lambda e: e.reciprocal(out=rs2.ap[0:nr, 0:1], in_=rs2.ap[0:nr, 0:1]))(), reads=[rs2.buf], writes=[rs2.buf])
            P.op("dve", (lambda j=j, nr=nr, y=y: lambda e: e.scalar_tensor_tensor(out=y.ap[0:nr, :], in0=ysb[j].ap[0:nr, :], scalar=rs2.ap[0:nr, 0:1],
                                                                               in1=g_fin_rep[0:nr, :], op0=ALU.mult, op1=ALU.mult))(),
                 reads=[ysb[j].buf, rs2.buf] + CONST, writes=[y.buf])
            P.op("sp", (lambda y=y, r0=r0, nr=nr: lambda e: e.dma_start(out=ydst[r0:r0 + nr, :], in_=y.ap[0:nr, :]))(),
                 reads=[y.buf], dsem=y.dsem)
        P.barrier()
        AR.limit = AR.cap
        AR.reset(m_slot)

    kbufs, vbufs = phase_A()
    for si in range(NSLOT):
        slot(si, False, kbufs, vbufs)
    slot(NSLOT, True, kbufs, vbufs)

    P.finalize()
    with nc.Block() as block:
        @block.tensor
        def _(e):
            P.emit("pe", e)

        @block.scalar
        def _(e):
            P.emit("act", e)

        @block.vector
        def _(e):
            P.emit("dve", e)

        @block.gpsimd
        def _(e):
            P.emit("pool", e)

        @block.sync
        def _(e):
            P.emit("sp", e)
            P.final_waits(e)
    stack.close()
    return nc


def make_consts(nf, w_conv, g_att_out, g_conv_out, g_mix, g_ffn, w_ffn_conv, g_final):
    W = 512 + HALO
    WS = DEC + HALO
    ident = np.eye(128, dtype=np.float32)
    ones = np.ones((128, 128), np.float32)
    j = np.arange(128)[:, None]
    s = np.arange(128)[None, :]
    negtri = np.where(j >= s, -1.0, 0.0).astype(np.float32)
    r = np.arange(128)[:, None]
    col = np.arange(W)[None, :]
    ms = []
    for mi in range(5):
        delta = (mi - 1) * 128
        ms.append((col > delta + r + HALO).astype(np.float32))
    cs = np.arange(WS)[None, :]
    msk_s = ((cs > r + HALO) & (r < DEC)).astype(np.float32)
    c_bf = np.concatenate([ident, ones, negtri] + ms + [msk_s], axis=1).astype(np.float32)
    pp = lambda v, n: np.ascontiguousarray(v.reshape(n, 128).T)
    parts = [ident,
             np.broadcast_to(g_mix.reshape(1, D), (128, D)),
             np.broadcast_to(g_final.reshape(1, D), (128, D)),
             pp(g_ffn.reshape(-1), KC),
             np.ascontiguousarray(g_att_out.reshape(8, 128).T),
             np.ascontiguousarray(g_conv_out.reshape(8, 128).T),
             np.ascontiguousarray(w_conv.reshape(3, 8, 128).transpose(2, 1, 0)).reshape(128, 24),
             np.ascontiguousarray(w_ffn_conv.reshape(3, nf, 128).transpose(2, 1, 0)).reshape(128, 3 * nf)]
    c_f32 = np.concatenate([np.asarray(p, np.float32) for p in parts], axis=1)
    return np.ascontiguousarray(c_bf), np.ascontiguousarray(c_f32)


_NC_CACHE = {}


def run(inputs, nslot, debug=False):
    f = lambda a: np.ascontiguousarray(np.asarray(a, dtype=np.float32))
    xp = f(inputs["x_prompt"])[0]
    xsm = f(inputs["x_sample"])
    ck = f(inputs["cache_k"])[0]
    cv = f(inputs["cache_v"])[0]
    stc = f(inputs["state_conv"])[0]
    stf = f(inputs["state_ffn_conv"])[0]
    w_in = f(inputs["w_in"])[0]
    w_o = f(inputs["w_o"])[0]
    w_gu = f(inputs["w_gate_up"])[0]
    w_dn = f(inputs["w_down"])[0]
    dff = w_dn.shape[0]
    nf = dff // 128
    seq = xp.shape[0]
    NT = 8 * nslot
    assert seq == NT * 512
    W = 512 + HALO
    WS = DEC + HALO
    c_bf, c_f32 = make_consts(nf, f(inputs["w_conv"])[0], f(inputs["g_att_out"])[0], f(inputs["g_conv_out"])[0],
                              f(inputs["g_mix"])[0], f(inputs["g_ffn"])[0], f(inputs["w_ffn_conv"])[0], f(inputs["g_final"]))
    key = (nslot, nf, debug)
    if key not in _NC_CACHE:
        _NC_CACHE[key] = build({"nslot": nslot, "nf": nf, "debug": debug})
    nc = _NC_CACHE[key]
    in_maps = []
    for c in range(NCORE):
        npad = (7 - c) * 512
        x_seq = np.zeros((NT * 512, D), np.float32)
        x_seq[npad:] = xp[:NT * 512 - npad]
        x_own = np.zeros((nslot, W, D), np.float32)
        for s in range(nslot):
            q0 = (8 * s + c) * 512
            lo = q0 - HALO
            if lo < 0:
                x_own[s, -lo:] = xp[0:q0 + 512]
            else:
                x_own[s] = xp[lo:q0 + 512]
        x_smp = np.zeros((WS, D), np.float32)
        x_smp[HALO:] = xsm[c]
        in_maps.append({
            "x_seq": x_seq, "x_own": x_own, "x_smp": x_smp,
            "cache_k": np.ascontiguousarray(ck[c]), "cache_v": np.ascontiguousarray(cv[c]),
            "st_conv": np.ascontiguousarray(stc[c].reshape(2, 8, 128).transpose(2, 1, 0)),
            "st_ffn": np.ascontiguousarray(stf[c].reshape(2, nf, 128).transpose(2, 1, 0)),
            "w_in": w_in, "w_o": w_o, "w_gu": w_gu, "w_dn": w_dn, "c_bf": c_bf, "c_f32": c_f32,
        })
    res = run_bass_kernel_spmd(nc, in_maps, core_ids=list(range(NCORE)))
    R = res.results
    if debug:
        run.dbg = [{k: np.asarray(r[k]) for k in ("dbg_mix", "dbg_xmid", "dbg_h2", "dbg_att", "dbg_nrm")} for r in R]
    y_p = np.zeros((1, seq, D), np.float32)
    k_p = np.zeros((1, 1, seq, H, HD), np.float32)
    v_p = np.zeros((1, 1, seq, H, HD), np.float32)
    y_s = np.zeros((NCORE, DEC, D), np.float32)
    k_s = np.zeros((1, NCORE, DEC, H, HD), np.float32)
    v_s = np.zeros((1, NCORE, DEC, H, HD), np.float32)
    c_s = np.zeros((1, NCORE, 2, 1024), np.float32)
    f_s = np.zeros((1, NCORE, 2, dff), np.float32)
    for c in range(NCORE):
        r = R[c]
        for s in range(nslot):
            t = 8 * s + c
            y_p[0, t * 512:(t + 1) * 512] = np.asarray(r["y_own"])[s]
            k_p[0, 0, t * 512:(t + 1) * 512] = np.asarray(r["k_own"])[s].reshape(512, H, HD)
            v_p[0, 0, t * 512:(t + 1) * 512] = np.asarray(r["v_own"])[s].reshape(512, H, HD)
        y_s[c] = np.asarray(r["y_smp"])
        k_s[0, c] = np.asarray(r["k_smp"]).reshape(DEC, H, HD)
        v_s[0, c] = np.asarray(r["v_smp"]).reshape(DEC, H, HD)
        cl = np.asarray(r["conv_last"])
        fl = np.asarray(r["ffn_last"])
        c_s[0, c] = cl[nslot].reshape(128, 8, 2).transpose(2, 1, 0).reshape(2, 1024)
        f_s[0, c] = fl[nslot].reshape(128, nf, 2).transpose(2, 1, 0).reshape(2, dff)
        if c == NCORE - 1:
            c_p = cl[nslot - 1].reshape(128, 8, 2).transpose(2, 1, 0).reshape(1, 1, 2, 1024).copy()
            f_p = fl[nslot - 1].reshape(128, nf, 2).transpose(2, 1, 0).reshape(1, 1, 2, dff).copy()
    return (y_p, y_s, k_p, v_p, c_p, f_p, k_s, v_s, c_s, f_s)


def kernel(**inputs):
    return run(inputs, 4)
```
